# Optimizing a Trainium2 kernel written in Bass

```python
import math
import jax, jax.numpy as jnp
from jax import lax
import numpy as np

D_MODEL = 1024
BATCH = 32
SEQ = 2048
DEPTH = 4

NORM_EPS = 1e-6
NEG_INF = -1e30
Q_BLOCK = 128
PLE_DIM = 256
D_FF = 2816
MLA_NOPE = 64
MLA_ROPE = 32
MLA_V = 64
MLA_HEADS = D_MODEL // MLA_V
MLA_Q_LORA = 384
MLA_KV_LORA = 256
ROPE_THETA = 10000.0
SWA_HEAD_DIM = 64
SWA_HEADS = D_MODEL // SWA_HEAD_DIM
SWA_KV_HEADS = 4
WINDOW = 128
DIFF_HEAD_DIM = 64
DIFF_HEADS = D_MODEL // (2 * DIFF_HEAD_DIM)
REL_BUCKETS = 32
REL_MAX_DIST = 128

IN_WIDTHS = (MLA_Q_LORA, MLA_KV_LORA, MLA_ROPE,
             SWA_HEADS * SWA_HEAD_DIM, SWA_KV_HEADS * SWA_HEAD_DIM, SWA_KV_HEADS * SWA_HEAD_DIM,
             DIFF_HEADS * 2 * DIFF_HEAD_DIM, DIFF_HEADS * 2 * DIFF_HEAD_DIM, DIFF_HEADS * 2 * DIFF_HEAD_DIM,
             D_MODEL, D_MODEL, D_MODEL)
IN_WIDTH = sum(IN_WIDTHS)
IN_SPLITS = tuple(int(v) for v in np.cumsum(IN_WIDTHS)[:-1])

kernel_name = 'hybrid_gated_mla_swa_diff_encoder'


def rms_norm(x, gain):
    x32 = x.astype(jnp.float32)
    y = x32 * lax.rsqrt(jnp.mean(x32 * x32, axis=-1, keepdims=True) + NORM_EPS)
    return (y * gain.astype(jnp.float32)).astype(x.dtype)


def swiglu(x, w_gate, w_up, w_down):
    return (jax.nn.silu(x @ w_gate) * (x @ w_up)) @ w_down


def t5_bucket(rel):
    nb = REL_BUCKETS // 2
    max_exact = nb // 2
    base = jnp.where(rel > 0, nb, 0)
    n = jnp.abs(rel)
    nf = jnp.maximum(n, 1).astype(jnp.float32)
    large = max_exact + (jnp.log(nf / max_exact) / math.log(REL_MAX_DIST / max_exact) * (nb - max_exact)).astype(jnp.int32)
    large = jnp.minimum(large, nb - 1)
    return base + jnp.where(n < max_exact, n, large)


def rel_bias(table, rel):
    return jnp.moveaxis(table[t5_bucket(rel)], -1, 0).astype(jnp.float32)


def rope_angles(seq_len):
    inv = ROPE_THETA ** (-jnp.arange(0, MLA_ROPE, 2, dtype=jnp.float32) / MLA_ROPE)
    ang = jnp.arange(seq_len, dtype=jnp.float32)[:, None] * inv[None, :]
    return jnp.cos(ang), jnp.sin(ang)


def apply_rope(x, cos, sin):
    x1, x2 = jnp.split(x.astype(jnp.float32), 2, axis=-1)
    return jnp.concatenate([x1 * cos - x2 * sin, x2 * cos + x1 * sin], axis=-1).astype(x.dtype)


def to_blocks(x):
    b, s = x.shape[:2]
    return jnp.moveaxis(x.reshape(b, s // Q_BLOCK, Q_BLOCK, *x.shape[2:]), 1, 0)


def from_blocks(y):
    y = jnp.moveaxis(y, 0, 1)
    return y.reshape(y.shape[0], y.shape[1] * y.shape[2], *y.shape[3:])


def mla_attention(c_q, c_kv, k_rope, q_norm, w_uq, kv_norm, w_ukv, cos, sin):
    b, s, _ = c_q.shape
    q = (rms_norm(c_q, q_norm) @ w_uq).reshape(b, s, MLA_HEADS, MLA_NOPE + MLA_ROPE)
    q_nope = q[..., :MLA_NOPE]
    q_rope = apply_rope(q[..., MLA_NOPE:], cos[:, None, :], sin[:, None, :])
    kv = (rms_norm(c_kv, kv_norm) @ w_ukv).reshape(b, s, MLA_HEADS, MLA_NOPE + MLA_V)
    k_nope, v = kv[..., :MLA_NOPE], kv[..., MLA_NOPE:]
    k_rope = apply_rope(k_rope, cos, sin)
    scale = (MLA_NOPE + MLA_ROPE) ** -0.5

    def block(args):
        qn, qr = args
        logits = (jnp.einsum('bqhd,bkhd->bhqk', qn, k_nope)
                  + jnp.einsum('bqhr,bkr->bhqk', qr, k_rope)).astype(jnp.float32) * scale
        probs = jax.nn.softmax(logits, axis=-1).astype(v.dtype)
        return jnp.einsum('bhqk,bkhd->bqhd', probs, v)

    out = lax.map(block, (to_blocks(q_nope), to_blocks(q_rope)))
    return from_blocks(out).reshape(b, s, MLA_HEADS * MLA_V)


def window_gqa(q, k, v, sink, bias_table):
    b, s = q.shape[:2]
    g = SWA_HEADS // SWA_KV_HEADS
    span = Q_BLOCK + 2 * WINDOW
    pad = ((0, 0), (WINDOW, WINDOW), (0, 0), (0, 0))
    kp, vp = jnp.pad(k, pad), jnp.pad(v, pad)
    qb = to_blocks(q.reshape(b, s, SWA_KV_HEADS, g, SWA_HEAD_DIM))
    scale = SWA_HEAD_DIM ** -0.5
    sink = sink.reshape(SWA_KV_HEADS, g).astype(jnp.float32)
    q_off = jnp.arange(Q_BLOCK)
    k_off = jnp.arange(span) - WINDOW

    def block(args):
        j, qj = args
        start = j * Q_BLOCK
        kj = lax.dynamic_slice_in_dim(kp, start, span, axis=1)
        vj = lax.dynamic_slice_in_dim(vp, start, span, axis=1)
        q_pos = start + q_off
        k_pos = start + k_off
        rel = k_pos[None, :] - q_pos[:, None]
        valid = (jnp.abs(rel) <= WINDOW) & (k_pos >= 0)[None, :] & (k_pos < s)[None, :]
        bias = rel_bias(bias_table, rel).reshape(SWA_KV_HEADS, g, Q_BLOCK, span)
        logits = jnp.einsum('bqkgd,bskd->bkgqs', qj, kj).astype(jnp.float32) * scale + bias
        logits = jnp.where(valid, logits, NEG_INF)
        sink_col = jnp.broadcast_to(sink[None, :, :, None, None], logits.shape[:-1] + (1,))
        probs = jax.nn.softmax(jnp.concatenate([logits, sink_col], axis=-1), axis=-1)[..., :span]
        return jnp.einsum('bkgqs,bskd->bqkgd', probs.astype(vj.dtype), vj)

    out = lax.map(block, (jnp.arange(s // Q_BLOCK), qb))
    return from_blocks(out).reshape(b, s, SWA_HEADS * SWA_HEAD_DIM)


def diff_attention(q, k, v, lq1, lk1, lq2, lk2, subln, lambda_init, bias_table):
    b, s = q.shape[:2]
    f32 = jnp.float32
    lam = (jnp.exp(jnp.sum(lq1.astype(f32) * lk1.astype(f32)))
           - jnp.exp(jnp.sum(lq2.astype(f32) * lk2.astype(f32))) + lambda_init)
    scale = DIFF_HEAD_DIM ** -0.5
    k_pos = jnp.arange(s)

    def block(args):
        j, qj = args
        q_pos = j * Q_BLOCK + jnp.arange(Q_BLOCK)
        bias = rel_bias(bias_table, k_pos[None, :] - q_pos[:, None])
        logits = jnp.einsum('bqhcd,bkhcd->bchqk', qj, k).astype(f32) * scale + bias
        probs = jax.nn.softmax(logits, axis=-1)
        attn = probs[:, 0] - lam * probs[:, 1]
        return jnp.einsum('bhqk,bkhe->bqhe', attn.astype(v.dtype), v)

    out = from_blocks(lax.map(block, (jnp.arange(s // Q_BLOCK), to_blocks(q))))
    out = rms_norm(out, subln) * (1.0 - lambda_init)
    return out.reshape(b, s, DIFF_HEADS * 2 * DIFF_HEAD_DIM)


def setup_inputs(seed: int = 0) -> dict:
    key = jax.random.key(seed)
    ks = iter(jax.random.split(key, 64))

    def normal(shape, scale):
        return jax.random.normal(next(ks), shape, jnp.float32) * scale

    def gain(shape):
        return 1.0 + normal(shape, 0.02)

    L, D, F = DEPTH, D_MODEL, D_FF
    dd = DIFF_HEAD_DIM
    return {
        'x': normal((BATCH, SEQ, D), 1.0),
        'p': normal((DEPTH, BATCH, SEQ, PLE_DIM), 1.0),
        'ffn1_norm': gain((L, D)),
        'ffn1_w_gate': normal((L, D, F), D ** -0.5),
        'ffn1_w_up': normal((L, D, F), D ** -0.5),
        'ffn1_w_down': normal((L, F, D), F ** -0.5),
        'mix_norm': gain((L, D)),
        'w_in': normal((L, D, IN_WIDTH), D ** -0.5),
        'mla_q_norm': gain((L, MLA_Q_LORA)),
        'mla_w_uq': normal((L, MLA_Q_LORA, MLA_HEADS * (MLA_NOPE + MLA_ROPE)), MLA_Q_LORA ** -0.5),
        'mla_kv_norm': gain((L, MLA_KV_LORA)),
        'mla_w_ukv': normal((L, MLA_KV_LORA, MLA_HEADS * (MLA_NOPE + MLA_V)), MLA_KV_LORA ** -0.5),
        'swa_sink': normal((L, SWA_HEADS), 0.5),
        'diff_lambda_q1': normal((L, dd), 0.1),
        'diff_lambda_k1': normal((L, dd), 0.1),
        'diff_lambda_q2': normal((L, dd), 0.1),
        'diff_lambda_k2': normal((L, dd), 0.1),
        'diff_subln': gain((L, 2 * dd)),
        'rel_table': normal((REL_BUCKETS, SWA_HEADS + DIFF_HEADS), 0.2),
        'w_out': normal((L, D, D), D ** -0.5),
        'ffn2_norm': gain((L, D)),
        'ffn2_w_gate': normal((L, D, F), D ** -0.5),
        'ffn2_w_up': normal((L, D, F), D ** -0.5),
        'ffn2_w_down': normal((L, F, D), F ** -0.5),
        'ple_norm': gain((L, D)),
        'ple_w_gate': normal((L, D, D), D ** -0.5),
        'ple_w_proj': normal((L, PLE_DIM, D), PLE_DIM ** -0.5),
        'final_norm': gain((D,)),
    }


def reference(x, p, ffn1_norm, ffn1_w_gate, ffn1_w_up, ffn1_w_down, mix_norm, w_in,
              mla_q_norm, mla_w_uq, mla_kv_norm, mla_w_ukv, swa_sink,
              diff_lambda_q1, diff_lambda_k1, diff_lambda_q2, diff_lambda_k2, diff_subln,
              rel_table, w_out, ffn2_norm, ffn2_w_gate, ffn2_w_up, ffn2_w_down,
              ple_norm, ple_w_gate, ple_w_proj, final_norm):
    b, s, _ = x.shape
    cos, sin = rope_angles(s)
    table_b = rel_table[:, :SWA_HEADS]
    table_c = rel_table[:, SWA_HEADS:]
    h = x
    for i in range(DEPTH):
        h = h + 0.5 * swiglu(rms_norm(h, ffn1_norm[i]), ffn1_w_gate[i], ffn1_w_up[i], ffn1_w_down[i])

        u = rms_norm(h, mix_norm[i])
        proj = u @ w_in[i]
        (c_q, c_kv, k_rope, q_b, k_b, v_b, q_c, k_c, v_c,
         g_a, g_b, g_c) = jnp.split(proj, IN_SPLITS, axis=-1)

        o_a = mla_attention(c_q, c_kv, k_rope, mla_q_norm[i], mla_w_uq[i],
                            mla_kv_norm[i], mla_w_ukv[i], cos, sin)
        o_b = window_gqa(q_b.reshape(b, s, SWA_HEADS, SWA_HEAD_DIM),
                         k_b.reshape(b, s, SWA_KV_HEADS, SWA_HEAD_DIM),
                         v_b.reshape(b, s, SWA_KV_HEADS, SWA_HEAD_DIM),
                         swa_sink[i], table_b)
        lambda_init = 0.8 - 0.6 * math.exp(-0.3 * i)
        o_c = diff_attention(q_c.reshape(b, s, DIFF_HEADS, 2, DIFF_HEAD_DIM),
                             k_c.reshape(b, s, DIFF_HEADS, 2, DIFF_HEAD_DIM),
                             v_c.reshape(b, s, DIFF_HEADS, 2 * DIFF_HEAD_DIM),
                             diff_lambda_q1[i], diff_lambda_k1[i], diff_lambda_q2[i], diff_lambda_k2[i],
                             diff_subln[i], lambda_init, table_c)

        merged = jax.nn.sigmoid(g_a) * o_a + jax.nn.sigmoid(g_b) * o_b + jax.nn.sigmoid(g_c) * o_c
        h = h + merged @ w_out[i]

        h = h + 0.5 * swiglu(rms_norm(h, ffn2_norm[i]), ffn2_w_gate[i], ffn2_w_up[i], ffn2_w_down[i])

        gate = jax.nn.sigmoid(rms_norm(h, ple_norm[i]) @ ple_w_gate[i])
        h = h + gate * (p[i] @ ple_w_proj[i])
    return rms_norm(h, final_norm)
```

```python
import math
from contextlib import ExitStack
import numpy as np
import concourse.bass as bass
import concourse.mybir as mybir
from concourse.bass_utils import run_bass_kernel_spmd

F32 = mybir.dt.float32
BF16 = mybir.dt.bfloat16
AF = mybir.ActivationFunctionType
ALU = mybir.AluOpType
AX = mybir.AxisListType

D = 1024; S = 2048; L = 4; B = 32; FF = 2816; PLE = 256
NCORE = 8; NSEQ = B // NCORE
EPS = 1e-6
NSLAB = 79
NCONST = 334
C_FFN1, C_MIX, C_FFN2, C_PLE, C_FIN, C_QN, C_KVN, C_SUBLN, C_SINK, C_CL, C_CR, C_LQ1, C_LK1, C_LQ2, C_LK2, C_SINK2 = \
    0, 8, 16, 24, 32, 40, 43, 45, 46, 54, 62, 70, 134, 198, 262, 326
GW = 1152
MIXSTOP = None
LAST_MARKS = []


class _Stop(Exception):
    pass


def _t5_bucket(rel):
    nb = 16; max_exact = 8
    try:
        import jax, jax.numpy as jnp
        with jax.default_device(jax.devices("cpu")[0]):
            r = jnp.asarray(rel.astype(np.int32))
            base = jnp.where(r > 0, nb, 0)
            n = jnp.abs(r)
            nf = jnp.maximum(n, 1).astype(jnp.float32)
            large = max_exact + (jnp.log(nf / max_exact) / math.log(128 / max_exact) * (nb - max_exact)).astype(jnp.int32)
            large = jnp.minimum(large, nb - 1)
            return np.asarray(base + jnp.where(n < max_exact, n, large))
    except Exception:
        base = np.where(rel > 0, nb, 0)
        n = np.abs(rel)
        nf = np.maximum(n, 1).astype(np.float32)
        large = max_exact + (np.log(nf / np.float32(max_exact)) / np.float32(math.log(128 / max_exact))
                             * np.float32(nb - max_exact)).astype(np.int32)
        large = np.minimum(large, nb - 1)
        return base + np.where(n < max_exact, n, large)


def _win_cols():
    cols = []
    ar = np.arange(128)
    for i in range(3): cols.append(ar + 128 * i)
    for i in range(2): cols.append(384 + ar + 128 * i)
    pad = np.zeros(32, np.int64)
    cols.append(np.concatenate([576 + np.arange(96), pad]))
    cols.append(np.concatenate([576 + np.arange(64), 656 + np.arange(16), 640 + np.arange(16), pad]))
    a64 = np.arange(64)
    for c in range(8):
        kv = c // 2
        cols.append(672 + 128 * c + ar)
        cols.append(np.concatenate([1696 + 64 * kv + a64] * 2))
        cols.append(np.concatenate([1952 + 64 * kv + a64] * 2))
        cols.append(2208 + 128 * c + ar)
        cols.append(3232 + 128 * c + ar)
        cols.append(4256 + 128 * c + ar)
        cols.append(5280 + 128 * c + ar)
        cols.append(6304 + 128 * c + ar)
        cols.append(7328 + 128 * c + ar)
    return np.stack(cols)


def _prep_shared(inp):
    f = np.float32
    out = {}
    cols = _win_cols().reshape(-1)
    w_in = inp["w_in"]
    wi = np.empty((L, NSLAB, 128, 8 * 128), f)
    for l in range(L):
        g = w_in[l][:, cols].reshape(8, 128, NSLAB, 128)
        wi[l] = g.transpose(2, 1, 0, 3).reshape(NSLAB, 128, 1024)
    out["wi"] = wi
    uq = inp["mla_w_uq"]
    ucols = []
    for c in range(8):
        nrm = []; swp = []
        for h in (2 * c, 2 * c + 1):
            nrm.append(96 * h + np.arange(96))
            swp.append(np.concatenate([96 * h + np.arange(64), 96 * h + 80 + np.arange(16), 96 * h + 64 + np.arange(16)]))
        ucols.append(np.concatenate(nrm + swp))
    ucols = np.stack(ucols)
    wuq = np.empty((L, 8, 128, 3 * 384), f)
    for l in range(L):
        g = uq[l][:, ucols.reshape(-1)].reshape(3, 128, 8, 384)
        wuq[l] = g.transpose(2, 1, 0, 3).reshape(8, 128, 3 * 384)
    out["wuq"] = wuq
    ukv = inp["mla_w_ukv"]
    out["wukv"] = np.ascontiguousarray(
        ukv.reshape(L, 2, 128, 8, 256).transpose(0, 3, 2, 1, 4).reshape(L, 8, 128, 512))
    wgu = np.empty((L, 2, 22, 128, 2048), f)
    wd = np.empty((L, 2, 2, 8, 128, 11 * 128), f)
    for w, (gn, un, dn) in enumerate((("ffn1_w_gate", "ffn1_w_up", "ffn1_w_down"),
                                      ("ffn2_w_gate", "ffn2_w_up", "ffn2_w_down"))):
        g = inp[gn].reshape(L, 8, 128, 22, 128).transpose(0, 3, 2, 1, 4)
        u = inp[un].reshape(L, 8, 128, 22, 128).transpose(0, 3, 2, 1, 4)
        wgu[:, w] = np.stack([g, u], axis=3).reshape(L, 22, 128, 2048)
        dd = inp[dn].reshape(L, 2, 11, 128, 8, 128).transpose(0, 1, 4, 3, 2, 5)
        wd[:, w] = dd.reshape(L, 2, 8, 128, 11 * 128)
    out["wgu"] = wgu
    out["wd"] = wd
    out["wout"] = np.ascontiguousarray(inp["w_out"])
    out["wpg"] = np.ascontiguousarray(
        inp["ple_w_gate"].reshape(L, 8, 128, 8, 128).transpose(0, 3, 2, 1, 4).reshape(L, 8, 128, 1024))
    out["wpp"] = np.ascontiguousarray(inp["ple_w_proj"])
    cst = np.zeros((L, 128, NCONST), f)
    for l in range(L):
        for nm, o in (("ffn1_norm", C_FFN1), ("mix_norm", C_MIX), ("ffn2_norm", C_FFN2), ("ple_norm", C_PLE)):
            cst[l, :, o:o + 8] = inp[nm][l].reshape(8, 128).T
        cst[l, :, C_FIN:C_FIN + 8] = inp["final_norm"].reshape(8, 128).T
        cst[l, :, C_QN:C_QN + 3] = inp["mla_q_norm"][l].reshape(3, 128).T
        cst[l, :, C_KVN:C_KVN + 2] = inp["mla_kv_norm"][l].reshape(2, 128).T
        cst[l, :, C_SUBLN] = inp["diff_subln"][l]
        sk = inp["swa_sink"][l]
        for c in range(8):
            cst[l, :64, C_SINK + c] = sk[2 * c]
            cst[l, 64:, C_SINK + c] = sk[2 * c + 1]
            cst[l, :64, C_SINK2 + c] = sk[2 * c + 1]
            cst[l, 64:, C_SINK2 + c] = sk[2 * c]
            cst[l, :, C_CL + c] = inp["rel_table"][15, 16 + c]
            cst[l, :, C_CR + c] = inp["rel_table"][31, 16 + c]
        for nm, o in (("diff_lambda_q1", C_LQ1), ("diff_lambda_k1", C_LK1), ("diff_lambda_q2", C_LQ2), ("diff_lambda_k2", C_LK2)):
            cst[l, :, o:o + 64] = inp[nm][l][None, :]
    out["cst"] = cst
    rel = np.arange(128)[:, None] - np.arange(GW)[None, :] + 512
    bk = _t5_bucket(rel)
    gm = np.ascontiguousarray(inp["rel_table"][bk].transpose(2, 0, 1)).astype(f)
    gm[:16][:, np.abs(rel) > 128] = -30000.0
    out["gm"] = gm
    inv = (10000.0 ** (-np.arange(0, 32, 2, dtype=np.float32) / np.float32(32))).astype(f)
    ang = (np.arange(S, dtype=f)[:, None] * inv[None, :]).astype(f)
    cs, sn = np.cos(ang).astype(f).T, np.sin(ang).astype(f).T
    tab = np.zeros((128, 2, S), f)
    tab[64:96, 0] = np.concatenate([cs, cs], 0)
    tab[64:96, 1] = np.concatenate([-sn, sn], 0)
    out["tab"] = tab
    return out


class KB:
    def __init__(self, nc, es):
        self.nc = nc; self.es = es
        self.eng = {"pe": nc.tensor, "act": nc.scalar, "dve": nc.vector, "pool": nc.gpsimd, "sp": nc.sync}
        self.sem = {e: es.enter_context(nc.semaphore("s_" + e)) for e in self.eng}
        self.cnt = {e: 0 for e in self.eng}
        self.known = {e: {} for e in self.eng}
        self.lastw = {}; self.readers = {}
        self.dsem = {}; self.dcnt = {}
        self._cur = None
        self.nmm = 0; self.marks = []

    def _deps(self, eng, reads, writes, is_dma):
        need = {}

        def add(rec, raw):
            if rec[0] == "c":
                e2 = rec[1]
                if e2 == eng and not is_dma:
                    if eng == "pe" or not raw:
                        return
            k = rec[:2]
            if need.get(k, 0) < rec[2]:
                need[k] = rec[2]
        for r in reads:
            w = self.lastw.get(r)
            if w: add(w, True)
            if type(r) is tuple and r[0] == "ps":
                rd = self.readers.get(r)
                if rd:
                    for rec in rd.values(): add(rec, False)
        for k in writes:
            w = self.lastw.get(k)
            if w: add(w, False)
            rd = self.readers.get(k)
            if rd:
                for rec in rd.values(): add(rec, False)
        kn = self.known[eng]
        e = self.eng[eng]
        for sk, v in need.items():
            if kn.get(sk, 0) < v:
                sem = self.sem[sk[1]] if sk[0] == "c" else self.dsem[sk[1]]
                e.wait_ge(sem, v)
                kn[sk] = v

    def _commit(self, rec, reads, writes):
        for r in reads:
            self.readers.setdefault(r, {})[rec[:2]] = rec
        for k in writes:
            self.lastw[k] = rec; self.readers[k] = {}

    def begin(self, eng, reads, writes):
        self._deps(eng, reads, writes, False)
        self._cur = (eng, reads, writes)
        return self.eng[eng]

    def end(self, ins):
        eng, reads, writes = self._cur
        self.cnt[eng] += 1
        ins.then_inc(self.sem[eng], 1)
        self._commit(("c", eng, self.cnt[eng]), reads, writes)

    def dma(self, q, key, reads, writes, pairs):
        if key not in self.dsem:
            self.dsem[key] = self.es.enter_context(self.nc.semaphore("d_" + str(len(self.dsem))))
            self.dcnt[key] = 0
        self._deps(q, reads, writes, True)
        e = self.eng[q]
        for o, i in pairs:
            e.dma_start(out=o, in_=i).then_inc(self.dsem[key], 16)
            self.dcnt[key] += 16
        self._commit(("d", key, self.dcnt[key]), reads, writes)

    def mm(self, e, *a, **kw):
        self.nmm += 1
        return e.matmul(*a, **kw)

    def mark(self, name):
        self.marks.append((name, self.nmm))

    def barrier(self):
        for e in self.eng:
            kn = self.known[e]
            for e2 in self.eng:
                if e2 == "pe" and e == "pe": continue
                if self.cnt[e2] > kn.get(("c", e2), 0):
                    self.eng[e].wait_ge(self.sem[e2], self.cnt[e2]); kn[("c", e2)] = self.cnt[e2]
            for key, v in self.dcnt.items():
                if v > kn.get(("d", key), 0):
                    self.eng[e].wait_ge(self.dsem[key], v); kn[("d", key)] = v
        self.lastw.clear(); self.readers.clear()


def build(nseq=NSEQ, depth=L, phases=("ffn1", "mix", "ffn2", "ple"), fin=True):
    nc = bass.Bass("TRN2", target_bir_lowering=False)

    def din(name, shape):
        return nc.dram_tensor(name, shape, F32, kind="ExternalInput").ap()
    xT = din("xT", [nseq, D, S]); pT = din("pT", [L, nseq, PLE, S])
    wi = din("wi", [L, NSLAB, 128, 1024]); wuq_d = din("wuq", [L, 8, 128, 1152]); wukv_d = din("wukv", [L, 8, 128, 512])
    wgu_d = din("wgu", [L, 2, 22, 128, 2048]); wd_d = din("wd", [L, 2, 2, 8, 128, 1408])
    wout_d = din("wout", [L, D, D]); wpg_d = din("wpg", [L, 8, 128, 1024]); wpp_d = din("wpp", [L, PLE, D])
    cst_d = din("cst", [L, 128, NCONST]); gm_d = din("gm", [24, 128, GW]); tab_d = din("tab", [128, 2, S])
    yT = nc.dram_tensor("yT", [nseq, D, S], F32, kind="ExternalOutput").ap()
    dbg = nc.dram_tensor("dbg", [128, S], F32, kind="ExternalOutput").ap() if MIXSTOP is not None else None
    dbgref = {}

    with ExitStack() as es:
        k = KB(nc, es)

        uid = [0]

        def sb(st, name, shape, dt):
            uid[0] += 1
            return st.enter_context(nc.sbuf_tensor("t%d_%s" % (uid[0], name), shape, dt))
        hT = sb(es, "hT", [128, 8, S], F32)
        uT = sb(es, "uT", [128, 8, S], BF16)
        cst = sb(es, "cst", [128, NCONST], F32)
        ones = sb(es, "ones", [128, 128], BF16)
        onesA = sb(es, "onesA", [128, 128], BF16)
        onesB = sb(es, "onesB", [128, 128], BF16)
        sq = sb(es, "sq", [128, 2, 512], BF16)
        rstd = sb(es, "rstd", [128, 512], F32)
        sgt = sb(es, "sgt", [128, 2, 512], F32)
        tmp = sb(es, "tmp", [128, 2, 512], F32)
        rs = sb(es, "rs", [128, 2, 512], F32)
        misc = sb(es, "misc", [128, 80], F32)
        ps = [es.enter_context(nc.psum_tensor("ps%d" % i, [128, 512], F32)) for i in range(8)]
        rot = {"gen": 0, "acc": 0, "sq": 0, "sgt": 0, "tmp": 0, "rs": 0}

        def gen():
            rot["gen"] = (rot["gen"] + 1) % 4; return rot["gen"]

        def acc():
            rot["acc"] = (rot["acc"] + 1) % 4; return 4 + rot["acc"]

        def nxt(name, n):
            rot[name] = (rot[name] + 1) % n; return rot[name]

        def tsl(tt):
            return slice(tt * 512, (tt + 1) * 512)

        e = k.begin("dve", [], ["ones"]); i_ = e.memset(ones[:], 1.0); k.end(i_)
        e = k.begin("dve", [], ["onesA"]); e.memset(onesA[:], 0.0); i_ = e.memset(onesA[:, 0:64], 1.0); k.end(i_)
        e = k.begin("dve", [], ["onesB"]); e.memset(onesB[:], 0.0); i_ = e.memset(onesB[:, 64:128], 1.0); k.end(i_)

        def mm_group(out_ap, pskey, pairs, reads):
            e = k.begin("pe", reads, [pskey])
            n = len(pairs)
            for i, (lh, rh) in enumerate(pairs):
                ins = k.mm(e, out_ap, lh, rh, start=(i == 0), stop=(i == n - 1))
            k.end(ins)

        def rstd_from(psb, dst, dkey):
            e = k.begin("act", [("ps", psb), "misc"], [dkey])
            ins = e.activation(out=dst, in_=ps[psb][:], func=AF.Ln, bias=misc[:, 8:9], scale=1.0); k.end(ins)
            e = k.begin("act", [dkey], [dkey])
            ins = e.activation(out=dst, in_=dst, func=AF.Exp, scale=-0.5); k.end(ins)

        e = k.begin("dve", [], ["misc"]); e.memset(misc[:], 0.0); e.memset(misc[:, 10:11], 1.0); i_ = e.memset(misc[:, 8:9], EPS); k.end(i_)

        k.barrier()

        def norm(gcol, out_fn):
            for tt in range(4):
                mb = acc()
                for kc in range(8):
                    r = nxt("sq", 2)
                    e = k.begin("act", [("hT", kc, tt)], [("sq", r)])
                    ins = e.activation(out=sq[:, r, :], in_=hT[:, kc, tsl(tt)], func=AF.Square, scale=1.0 / 32.0); k.end(ins)
                    e = k.begin("pe", [("sq", r), "ones"], [("ps", mb)])
                    ins = k.mm(e, ps[mb][:], ones[:], sq[:, r, :], start=(kc == 0), stop=(kc == 7)); k.end(ins)
                rstd_from(mb, rstd[:], "rstd")
                for kc in range(8):
                    out_fn(kc, tt)

        def norm_to_u(gcol):
            def f(kc, tt):
                e = k.begin("dve", [("hT", kc, tt), "rstd", "cst"], [("uT", tt)])
                ins = e.scalar_tensor_tensor(out=uT[:, kc, tsl(tt)], in0=hT[:, kc, tsl(tt)],
                                             scalar=cst[:, gcol + kc:gcol + kc + 1], in1=rstd[:],
                                             op0=ALU.mult, op1=ALU.mult); k.end(ins)
            norm(gcol, f)

        uT_all = [("uT", t) for t in range(4)]

        def ffn(l, which, gcol):
            k.mark('ffn%d' % which)
            with ExitStack() as ph:
                aT = sb(ph, "aT", [128, 11, S], BF16)
                wgu = sb(ph, "wgu", [128, 2, 2048], BF16)
                wd = sb(ph, "wd", [128, 2, 1408], BF16)
                norm_to_u(gcol)
                for half in range(2):
                    for j in range(11):
                        fc = half * 11 + j; bf = fc % 2
                        k.dma("pool", ("wgu", bf), [], [("wgu", bf)], [(wgu[:, bf, :], wgu_d[l, which, fc])])
                        for tt in range(4):
                            gb = gen(); ub = acc()
                            mm_group(ps[gb][:], ("ps", gb),
                                     [(wgu[:, bf, kc * 128:(kc + 1) * 128], uT[:, kc, tsl(tt)]) for kc in range(8)],
                                     [("wgu", bf), ("uT", tt)])
                            mm_group(ps[ub][:], ("ps", ub),
                                     [(wgu[:, bf, 1024 + kc * 128:1024 + (kc + 1) * 128], uT[:, kc, tsl(tt)]) for kc in range(8)],
                                     [("wgu", bf), ("uT", tt)])
                            r = nxt("sgt", 2)
                            e = k.begin("act", [("ps", gb)], [("sgt", r)])
                            ins = e.activation(out=sgt[:, r, :], in_=ps[gb][:], func=AF.Silu); k.end(ins)
                            e = k.begin("dve", [("ps", ub), ("sgt", r)], [("aT", j, tt)])
                            ins = e.tensor_tensor(out=aT[:, j, tsl(tt)], in0=ps[ub][:], in1=sgt[:, r, :], op=ALU.mult); k.end(ins)
                    for dc in range(8):
                        bf = dc % 2
                        k.dma("pool", ("wd", bf), [], [("wd", bf)], [(wd[:, bf, :], wd_d[l, which, half, dc])])
                        for tt in range(4):
                            ob = gen()
                            mm_group(ps[ob][:], ("ps", ob),
                                     [(wd[:, bf, j * 128:(j + 1) * 128], aT[:, j, tsl(tt)]) for j in range(11)],
                                     [("wd", bf)] + [("aT", j, tt) for j in range(11)])
                            e = k.begin("dve", [("ps", ob), ("hT", dc, tt)], [("hT", dc, tt)])
                            ins = e.scalar_tensor_tensor(out=hT[:, dc, tsl(tt)], in0=ps[ob][:], scalar=0.5,
                                                         in1=hT[:, dc, tsl(tt)], op0=ALU.mult, op1=ALU.add); k.end(ins)
                k.barrier()

        def ple(l, s):
            k.mark('ple')
            with ExitStack() as ph:
                pTt = sb(ph, "pTt", [128, 2, S], BF16)
                wpp = sb(ph, "wpp", [128, 2, D], BF16)
                wpg = sb(ph, "wpg", [128, 2, 1024], BF16)
                k.dma("pool", "pTt", [], ["pTt"], [(pTt[:, :, :], pT[l, s].rearrange("(c p) t -> p c t", p=128))])
                k.dma("pool", "wpp", [], ["wpp"], [(wpp[:, :, :], wpp_d[l].rearrange("(c p) n -> p c n", p=128))])
                norm_to_u(C_PLE)
                for dc in range(8):
                    bf = dc % 2
                    k.dma("pool", ("wpg", bf), [], [("wpg", bf)], [(wpg[:, bf, :], wpg_d[l, dc])])
                    for tt in range(4):
                        gb = gen(); pb = acc()
                        mm_group(ps[gb][:], ("ps", gb),
                                 [(wpg[:, bf, kc * 128:(kc + 1) * 128], uT[:, kc, tsl(tt)]) for kc in range(8)],
                                 [("wpg", bf), ("uT", tt)])
                        mm_group(ps[pb][:], ("ps", pb),
                                 [(wpp[:, kc, dc * 128:(dc + 1) * 128], pTt[:, kc, tsl(tt)]) for kc in range(2)],
                                 ["wpp", "pTt"])
                        r = sigmoid_gate(gb)
                        r2 = nxt("tmp", 2)
                        e = k.begin("dve", [("ps", pb), ("sgt", r)], [("tmp", r2)])
                        ins = e.tensor_tensor(out=tmp[:, r2, :], in0=ps[pb][:], in1=sgt[:, r, :], op=ALU.mult); k.end(ins)
                        e = k.begin("dve", [("tmp", r2), ("hT", dc, tt)], [("hT", dc, tt)])
                        ins = e.tensor_tensor(out=hT[:, dc, tsl(tt)], in0=tmp[:, r2, :], in1=hT[:, dc, tsl(tt)], op=ALU.add); k.end(ins)
                k.barrier()

        def sigmoid_gate(gb):
            r = nxt("sgt", 2)
            e = k.begin("act", [("ps", gb)], [("sgt", r)])
            ins = e.activation(out=sgt[:, r, :], in_=ps[gb][:], func=AF.Exp, scale=-1.0); k.end(ins)
            e = k.begin("act", [("sgt", r), "misc"], [("sgt", r)])
            ins = e.activation(out=sgt[:, r, :], in_=sgt[:, r, :], func=AF.Ln, bias=misc[:, 10:11], scale=1.0); k.end(ins)
            e = k.begin("act", [("sgt", r)], [("sgt", r)])
            ins = e.activation(out=sgt[:, r, :], in_=sgt[:, r, :], func=AF.Exp, scale=-1.0); k.end(ins)
            return r

        def mixer(l):
            k.mark('mixer')
            ph = ExitStack()
            try:
                _mixer(l, ph)
            except _Stop:
                if dbg is not None and "macc" in dbgref:
                    k.dma("sp", "dbg", [("macc", t) for t in range(4)], [], [(dbg[:, :], dbgref["macc"][:, :])])
            k.barrier()
            ph.close()

        def _mixer(l, ph):
            lam_init = 0.8 - 0.6 * math.exp(-0.3 * l)
            if True:
                cqn = sb(ph, "cqn", [128, 3, S], BF16)
                ckvn = sb(ph, "ckvn", [128, 2, S], BF16)
                kr = sb(ph, "kr", [128, S], BF16)
                tab = sb(ph, "tab", [128, 2, S], BF16)
                kA = sb(ph, "kA", [128, S], BF16)
                kBt = sb(ph, "kBt", [128, S], BF16)
                qJ = sb(ph, "qJ", [128, 2, 2, 512], BF16)
                Vt = sb(ph, "Vt", [128, 16, 192], BF16)
                G = sb(ph, "G", [128, 2, GW], F32)
                PT = sb(ph, "PT", [128, 4, 512], BF16)
                macc = sb(ph, "macc", [128, S], F32)
                dbgref["macc"] = macc
                o1 = sb(ph, "o1", [128, 512], F32)
                mt = sb(ph, "mt", [128, 1, 512], BF16)
                wsl = sb(ph, "wsl", [128, 4, 1024], BF16)
                wuq = sb(ph, "wuq", [128, 3, 384], BF16)
                wukv = sb(ph, "wukv", [128, 2, 2, 128], BF16)
                wout = sb(ph, "wout", [128, 2, D], BF16)
                rot.update({"wsl": 0, "pt": 0, "mt": 0, "wout": 0, "qj": 0})

                def chk(n):
                    if MIXSTOP == n:
                        raise _Stop()

                k.dma("pool", "tab", [], ["tab"], [(tab[:, :, :], tab_d[:, :, :])])
                norm_to_u(C_MIX)

                e = k.begin("dve", ["cst"], ["lamt"])
                ins = e.tensor_tensor(out=tmp[:, 0, 0:64], in0=cst[:, C_LQ1:C_LQ1 + 64], in1=cst[:, C_LK1:C_LK1 + 64], op=ALU.mult)
                ins = e.tensor_tensor(out=tmp[:, 0, 64:128], in0=cst[:, C_LQ2:C_LQ2 + 64], in1=cst[:, C_LK2:C_LK2 + 64], op=ALU.mult)
                k.end(ins)
                e = k.begin("dve", ["lamt"], ["lam1"])
                ins = e.reduce_sum(out=misc[:, 0:1], in_=tmp[:, 0, 0:64], axis=AX.X)
                ins = e.reduce_sum(out=misc[:, 1:2], in_=tmp[:, 0, 64:128], axis=AX.X); k.end(ins)
                e = k.begin("act", ["lam1"], ["lam2"])
                ins = e.activation(out=misc[:, 2:4], in_=misc[:, 0:2], func=AF.Exp); k.end(ins)
                e = k.begin("dve", ["lam2"], ["lam3"])
                ins = e.tensor_tensor(out=misc[:, 4:5], in0=misc[:, 3:4], in1=misc[:, 2:3], op=ALU.subtract); k.end(ins)
                e = k.begin("dve", ["lam3"], ["neglam"])
                ins = e.tensor_scalar(out=misc[:, 4:5], in0=misc[:, 4:5], scalar1=-lam_init, scalar2=None, op0=ALU.add); k.end(ins)
                e = k.begin("act", ["cst"], ["esk"])
                e.activation(out=misc[:, 24:32], in_=cst[:, C_SINK2:C_SINK2 + 8], func=AF.Exp)
                ins = e.activation(out=misc[:, 16:24], in_=cst[:, C_SINK:C_SINK + 8], func=AF.Exp); k.end(ins)
                chk(1)

                def load_slab(idx):
                    r = nxt("wsl", 4)
                    k.dma("pool", ("wsl", r), [], [("wsl", r)], [(wsl[:, r, :], wi[l, idx])])
                    return r

                def slab_mm(psb, r, tt, m=128):
                    mm_group(ps[psb][0:m, :], ("ps", psb),
                             [(wsl[:, r, kc * 128:kc * 128 + m], uT[:, kc, tsl(tt)]) for kc in range(8)],
                             [("wsl", r), ("uT", tt)])

                def latent(slab0, n, gcol, dst, dname):
                    rr = [load_slab(slab0 + i) for i in range(n)]
                    for tt in range(4):
                        bs = [gen() for _ in range(n)]
                        mb = acc()
                        for i in range(n):
                            slab_mm(bs[i], rr[i], tt)
                        for i in range(n):
                            r = nxt("sq", 2)
                            e = k.begin("act", [("ps", bs[i])], [("sq", r)])
                            ins = e.activation(out=sq[:, r, :], in_=ps[bs[i]][:], func=AF.Square,
                                               scale=1.0 / math.sqrt(128.0 * n)); k.end(ins)
                            e = k.begin("pe", [("sq", r), "ones"], [("ps", mb)])
                            ins = k.mm(e, ps[mb][:], ones[:], sq[:, r, :], start=(i == 0), stop=(i == n - 1)); k.end(ins)
                        rstd_from(mb, rstd[:], "rstd")
                        for i in range(n):
                            e = k.begin("dve", [("ps", bs[i]), "rstd", "cst"], [(dname, tt)])
                            ins = e.scalar_tensor_tensor(out=dst[:, i, tsl(tt)], in0=ps[bs[i]][:],
                                                         scalar=cst[:, gcol + i:gcol + i + 1], in1=rstd[:],
                                                         op0=ALU.mult, op1=ALU.mult); k.end(ins)
                k.mark('latent')
                latent(0, 3, C_QN, cqn, "cqn")
                latent(3, 2, C_KVN, ckvn, "ckvn")
                chk(2)

                def rope_rows(pa, pb_, out_ap, okey, tt):
                    r1 = nxt("tmp", 2)
                    e = k.begin("dve", [("ps", pa), "tab"], [("tmp", r1)])
                    ins = e.tensor_tensor(out=tmp[64:96, r1, :], in0=ps[pa][64:96, :], in1=tab[64:96, 0, tsl(tt)], op=ALU.mult); k.end(ins)
                    r2 = nxt("tmp", 2)
                    e = k.begin("dve", [("ps", pb_), "tab"], [("tmp", r2)])
                    ins = e.tensor_tensor(out=tmp[64:96, r2, :], in0=ps[pb_][64:96, :], in1=tab[64:96, 1, tsl(tt)], op=ALU.mult); k.end(ins)
                    e = k.begin("dve", [("tmp", r1), ("tmp", r2)], [okey])
                    ins = e.tensor_tensor(out=out_ap, in0=tmp[64:96, r1, :], in1=tmp[64:96, r2, :], op=ALU.add); k.end(ins)

                r_kr = load_slab(5); r_krs = load_slab(6)
                for tt in range(4):
                    pa = gen(); pb_ = gen()
                    slab_mm(pa, r_kr, tt, 96); slab_mm(pb_, r_krs, tt, 96)
                    rope_rows(pa, pb_, kr[64:96, tsl(tt)], ("kr", tt), tt)
                kr_all = [("kr", t) for t in range(4)]
                chk(3)

                def attention(streams, kbs, qt, scale, bias_fn):
                    items = [(kb, st) for kb in kbs for st in streams]
                    n = len(items)
                    LA = 3
                    pend = []
                    for i in range(n + LA):
                        if i < n:
                            kb, st = items[i]
                            sbk = gen()
                            mm_group(ps[sbk][:], ("ps", sbk), [(st["k"](kb), st["q"])], st["kreads"](kb) + st["qreads"])
                            p = nxt("pt", 4)
                            bmode = bias_fn(kb, st)
                            if bmode[0] == "G":
                                gi, c0 = bmode[1], bmode[2]
                                r = nxt("tmp", 2)
                                e = k.begin("dve", [("ps", sbk), ("G", gi)], [("tmp", r)])
                                ins = e.scalar_tensor_tensor(out=tmp[:, r, :], in0=ps[sbk][:], scalar=scale,
                                                             in1=G[:, gi, c0:c0 + 512], op0=ALU.mult, op1=ALU.add); k.end(ins)
                                e = k.begin("act", [("tmp", r)], [("pt", p)])
                                ins = e.activation(out=PT[:, p, :], in_=tmp[:, r, :], func=AF.Exp); k.end(ins)
                            elif bmode[0] == "C":
                                e = k.begin("act", [("ps", sbk), "cst"], [("pt", p)])
                                ins = e.activation(out=PT[:, p, :], in_=ps[sbk][:], func=AF.Exp, bias=bmode[1], scale=scale); k.end(ins)
                            else:
                                e = k.begin("act", [("ps", sbk)], [("pt", p)])
                                ins = e.activation(out=PT[:, p, :], in_=ps[sbk][:], func=AF.Exp, scale=scale); k.end(ins)
                            pend.append(p)
                        if i >= LA:
                            j = i - LA
                            kb, st = items[j]; p = pend[j]
                            first = (kb == kbs[0]); last = (kb == kbs[-1])
                            e = k.begin("pe", [("pt", p)] + st["vreads"](kb), [("ps", b_) for b_, _ in st["acc"]])
                            for b_, lf in st["acc"]:
                                ins = k.mm(e, ps[b_][:], lf(kb), PT[:, p, :], start=first, stop=last)
                            k.end(ins)

                def gate_for(rg, tt):
                    gb = gen()
                    slab_mm(gb, rg, tt)
                    return sigmoid_gate(gb)

                def finish(X, Y, sg_r, tt, mode, sinkcol=None):
                    r = nxt("rs", 2)
                    bcol = sinkcol if sinkcol is not None else 9
                    e = k.begin("act", [("ps", Y), "esk", "misc"], [("rs", r)])
                    ins = e.activation(out=rs[:, r, :], in_=ps[Y][:], func=AF.Ln, bias=misc[:, bcol:bcol + 1], scale=1.0); k.end(ins)
                    e = k.begin("act", [("rs", r)], [("rs", r)])
                    ins = e.activation(out=rs[:, r, :], in_=rs[:, r, :], func=AF.Exp, scale=-1.0); k.end(ins)
                    r2 = nxt("tmp", 2)
                    e = k.begin("dve", [("ps", X), ("rs", r)], [("tmp", r2)])
                    ins = e.tensor_tensor(out=tmp[:, r2, :], in0=ps[X][:], in1=rs[:, r, :], op=ALU.mult); k.end(ins)
                    if mode == "raw":
                        return r2
                    if mode == "set":
                        e = k.begin("dve", [("tmp", r2), ("sgt", sg_r)], [("macc", tt)])
                        ins = e.tensor_tensor(out=macc[:, tsl(tt)], in0=tmp[:, r2, :], in1=sgt[:, sg_r, :], op=ALU.mult); k.end(ins)
                    else:
                        e = k.begin("dve", [("tmp", r2), ("sgt", sg_r)], [("tmp", r2)])
                        ins = e.tensor_tensor(out=tmp[:, r2, :], in0=tmp[:, r2, :], in1=sgt[:, sg_r, :], op=ALU.mult); k.end(ins)
                        e = k.begin("dve", [("tmp", r2), ("macc", tt)], [("macc", tt)])
                        ins = e.tensor_tensor(out=macc[:, tsl(tt)], in0=tmp[:, r2, :], in1=macc[:, tsl(tt)], op=ALU.add); k.end(ins)
                    return r2

                def finish_pair(XA, XB, sg_r, tt, mode, sinkcol=None):
                    r = nxt("rs", 2)
                    bcol = sinkcol if sinkcol is not None else 9
                    e = k.begin("act", [("ps", XA), "esk", "misc"], [("rs", r)])
                    ins = e.activation(out=rs[64:128, r, :], in_=ps[XA][64:128, :], func=AF.Ln, bias=misc[64:128, bcol:bcol + 1], scale=1.0); k.end(ins)
                    e = k.begin("act", [("ps", XB), "esk", "misc"], [("rs", r)])
                    ins = e.activation(out=rs[0:64, r, :], in_=ps[XB][0:64, :], func=AF.Ln, bias=misc[0:64, bcol:bcol + 1], scale=1.0); k.end(ins)
                    e = k.begin("act", [("rs", r)], [("rs", r)])
                    ins = e.activation(out=rs[:, r, :], in_=rs[:, r, :], func=AF.Exp, scale=-1.0); k.end(ins)
                    r2 = nxt("tmp", 2)
                    e = k.begin("dve", [("ps", XA), ("rs", r)], [("tmp", r2)])
                    ins = e.tensor_tensor(out=tmp[0:64, r2, :], in0=ps[XA][0:64, :], in1=rs[64:128, r, :], op=ALU.mult); k.end(ins)
                    e = k.begin("dve", [("ps", XB), ("rs", r)], [("tmp", r2)])
                    ins = e.tensor_tensor(out=tmp[64:128, r2, :], in0=ps[XB][64:128, :], in1=rs[0:64, r, :], op=ALU.mult); k.end(ins)
                    if mode == "set":
                        e = k.begin("dve", [("tmp", r2), ("sgt", sg_r)], [("macc", tt)])
                        ins = e.tensor_tensor(out=macc[:, tsl(tt)], in0=tmp[:, r2, :], in1=sgt[:, sg_r, :], op=ALU.mult); k.end(ins)
                    else:
                        e = k.begin("dve", [("tmp", r2), ("sgt", sg_r)], [("tmp", r2)])
                        ins = e.tensor_tensor(out=tmp[:, r2, :], in0=tmp[:, r2, :], in1=sgt[:, sg_r, :], op=ALU.mult); k.end(ins)
                        e = k.begin("dve", [("tmp", r2), ("macc", tt)], [("macc", tt)])
                        ins = e.tensor_tensor(out=macc[:, tsl(tt)], in0=tmp[:, r2, :], in1=macc[:, tsl(tt)], op=ALU.add); k.end(ins)

                def v_evac(pb_, g, padded):
                    pv3 = ps[pb_][:, :].rearrange("p (a b) -> p a b", b=128)
                    if padded:
                        e = k.begin("dve", [("ps", pb_)], [("V", g)])
                        e.tensor_copy(out=Vt[:, g * 4:(g + 1) * 4, 0:64], in_=pv3[:, :, 0:64])
                        ins = e.tensor_copy(out=Vt[:, g * 4:(g + 1) * 4, 128:192], in_=pv3[:, :, 64:128]); k.end(ins)
                    else:
                        e = k.begin("dve", [("ps", pb_)], [("V", g)])
                        ins = e.tensor_copy(out=Vt[:, g * 4:(g + 1) * 4, 0:128], in_=pv3); k.end(ins)

                V_all = [("V", g) for g in range(4)]

                for c in range(8):
                    base = 7 + 9 * c
                    k.mark('mla%d' % c)
                    k.dma("pool", "wuq", [], ["wuq"], [(wuq[:, :, :], wuq_d[l, c].rearrange("p (a b) -> p a b", b=384))])
                    k.dma("pool", "wukv", [], ["wukv"], [(wukv[:, :, :, :], wukv_d[l, c].rearrange("p (a b c) -> p a b c", b=2, c=128))])
                    r_ga = load_slab(base + 6)
                    e = k.begin("dve", [], V_all)
                    ins = e.memset(Vt[:, :, 64:128], 1.0); k.end(ins)
                    e = k.begin("dve", [], ["kAr", "kBr"] + [("kA", t) for t in range(4)] + [("kB", t) for t in range(4)]
                                + [(a, b, c_) for a in ("qJn", "qJr") for b in (0, 1) for c_ in (0, 1)])
                    e.memset(kA[96:128, :], 0.0); e.memset(kBt[96:128, :], 0.0)
                    ins = e.memset(qJ[96:128, :, :, :], 0.0); k.end(ins)
                    for kt, kn in ((kA, "kA"), (kBt, "kB")):
                        e = k.begin("dve", kr_all, [kn + "r"] + [(kn, t) for t in range(4)])
                        ins = e.tensor_copy(out=kt[64:96, :], in_=kr[64:96, :]); k.end(ins)
                    for tt in range(4):
                        for hd, kt, kn in ((0, kA, "kA"), (1, kBt, "kB")):
                            pb_ = gen()
                            mm_group(ps[pb_][0:64, :], ("ps", pb_),
                                     [(wukv[:, kc, hd, 0:64], ckvn[:, kc, tsl(tt)]) for kc in range(2)],
                                     ["wukv", ("ckvn", tt)])
                            e = k.begin("dve", [("ps", pb_)], [(kn, tt)])
                            ins = e.tensor_copy(out=kt[0:64, tsl(tt)], in_=ps[pb_][0:64, :]); k.end(ins)
                    for g in range(4):
                        pb_ = gen()
                        e = k.begin("pe", ["wukv", ("ckvn", g)], [("ps", pb_)])
                        for tb in range(4):
                            t0 = (g * 4 + tb) * 128
                            for kc in range(2):
                                ins = k.mm(e, ps[pb_][:, tb * 128:(tb + 1) * 128].rearrange("p (a b) -> p a b", b=64),
                                               ckvn[:, kc, t0:t0 + 128],
                                               wukv[:, kc, :, 64:128], start=(kc == 0), stop=(kc == 1))
                        k.end(ins)
                        v_evac(pb_, g, True)
                    if c == 0: chk(4)

                    def mla_q(qt, jb):
                        for hd in (0, 1):
                            pa = gen(); pb_ = gen()
                            mm_group(ps[pa][0:96, :], ("ps", pa),
                                     [(wuq[:, kc, hd * 96:hd * 96 + 96], cqn[:, kc, tsl(qt)]) for kc in range(3)],
                                     ["wuq", ("cqn", qt)])
                            mm_group(ps[pb_][0:96, :], ("ps", pb_),
                                     [(wuq[:, kc, 192 + hd * 96:192 + hd * 96 + 96], cqn[:, kc, tsl(qt)]) for kc in range(3)],
                                     ["wuq", ("cqn", qt)])
                            e = k.begin("dve", [("ps", pa)], [("qJn", jb, hd)])
                            ins = e.tensor_copy(out=qJ[0:64, jb, hd, :], in_=ps[pa][0:64, :]); k.end(ins)
                            rope_rows(pa, pb_, qJ[64:96, jb, hd, :], ("qJr", jb, hd), qt)

                    k.mark('mla_att%d' % c)
                    mla_q(0, 0)
                    if c == 0: chk(5)
                    for qt in range(4):
                        jb = qt % 2
                        if qt < 3: mla_q(qt + 1, (qt + 1) % 2)
                        sg_r = gate_for(r_ga, qt)
                        XA = 4 + 2 * (qt % 2); XB = XA + 1
                        streams = []
                        for hd, kt, kn, xb_, c0 in ((0, kA, "kA", XA, 0), (1, kBt, "kB", XB, 64)):
                            streams.append({
                                "k": (lambda kb, kt=kt: kt[:, kb * 128:(kb + 1) * 128]),
                                "q": qJ[:, jb, hd, :],
                                "kreads": (lambda kb, kn=kn: [(kn, kb // 4), kn + "r"]),
                                "qreads": [("qJn", jb, hd), ("qJr", jb, hd)],
                                "vreads": (lambda kb: [("V", kb // 4)]),
                                "acc": [(xb_, (lambda kb, c0=c0: Vt[:, kb, c0:c0 + 128]))],
                            })
                        attention(streams, list(range(16)), qt, 96.0 ** -0.5, lambda kb, st: ("N",))
                        finish_pair(XA, XB, sg_r, qt, "set")
                        if c == 0 and qt == 0: chk(6)
                    if c == 0: chk(7)

                    def kv_proj(r_k, r_v, padded):
                        for tt in range(4):
                            pb_ = gen()
                            slab_mm(pb_, r_k, tt)
                            e = k.begin("dve", [("ps", pb_)], [("kA", tt)])
                            ins = e.tensor_copy(out=kA[:, tsl(tt)], in_=ps[pb_][:]); k.end(ins)
                        for g in range(4):
                            pb_ = gen()
                            e = k.begin("pe", [("wsl", r_v), ("uT", g)], [("ps", pb_)])
                            for tb in range(4):
                                t0 = (g * 4 + tb) * 128
                                for kc in range(8):
                                    ins = k.mm(e, ps[pb_][:, tb * 128:(tb + 1) * 128], uT[:, kc, t0:t0 + 128],
                                                   wsl[:, r_v, kc * 128:(kc + 1) * 128], start=(kc == 0), stop=(kc == 7))
                            k.end(ins)
                            v_evac(pb_, g, padded)

                    def q_proj(r_q, qt, jb):
                        pb_ = gen()
                        slab_mm(pb_, r_q, qt)
                        e = k.begin("dve", [("ps", pb_)], [("qJn", jb, 0), ("qJr", jb, 0), ("qJn", jb, 1), ("qJr", jb, 1)])
                        e.tensor_copy(out=qJ[0:64, jb, 0, :], in_=ps[pb_][0:64, :])
                        ins = e.tensor_copy(out=qJ[64:128, jb, 1, :], in_=ps[pb_][64:128, :]); k.end(ins)

                    def half_stream(lo, jb, acc_):
                        slot = 0 if lo == 0 else 1
                        return {
                            "k": (lambda kb: kA[:, kb * 128:(kb + 1) * 128]),
                            "q": qJ[:, jb, slot, :],
                            "kreads": (lambda kb: [("kA", kb // 4)]),
                            "qreads": [("qJn", jb, slot), ("qJr", jb, slot)],
                            "vreads": (lambda kb: [("V", kb // 4)]),
                            "acc": acc_, "lo": lo,
                        }

                    def q_zero():
                        e = k.begin("dve", [], [(a, b, c_) for a in ("qJn", "qJr") for b in (0, 1) for c_ in (0, 1)])
                        e.memset(qJ[64:128, :, 0, :], 0.0)
                        ins = e.memset(qJ[0:64, :, 1, :], 0.0); k.end(ins)

                    k.mark('swa%d' % c)
                    r_q = load_slab(base + 0); r_k = load_slab(base + 1); r_v = load_slab(base + 2)
                    k.dma("sp", ("G", 0), [], [("G", 0)], [(G[:, 0, :], gm_d[2 * c])])
                    k.dma("sp", ("G", 1), [], [("G", 1)], [(G[:, 1, :], gm_d[2 * c + 1])])
                    kv_proj(r_k, r_v, True)
                    if c == 0: chk(8)
                    q_zero()
                    r_gb = load_slab(base + 7)
                    q_proj(r_q, 0, 0)
                    for qt in range(4):
                        jb = qt % 2
                        if qt < 3: q_proj(r_q, qt + 1, (qt + 1) % 2)
                        sg_r = gate_for(r_gb, qt)
                        XA = 4 + 2 * (qt % 2); XB = XA + 1
                        streams = [half_stream(0, jb, [(XA, lambda kb: Vt[:, kb, 0:128])]),
                                   half_stream(64, jb, [(XB, lambda kb: Vt[:, kb, 64:192])])]
                        kbs = [kb for kb in range(4 * qt - 1, 4 * qt + 5) if 0 <= kb < 16]
                        attention(streams, kbs, qt, 0.125,
                                  lambda kb, st, qt=qt: ("G", 0 if st["lo"] == 0 else 1, 512 - 128 * (kb - 4 * qt)))
                        finish_pair(XA, XB, sg_r, qt, "add", sinkcol=24 + c)
                    if c == 0: chk(9)

                    k.mark('diff%d' % c)
                    r_q = load_slab(base + 3); r_k = load_slab(base + 4); r_v = load_slab(base + 5)
                    k.dma("sp", ("G", 0), [], [("G", 0)], [(G[:, 0, :], gm_d[16 + c])])
                    kv_proj(r_k, r_v, False)
                    if c == 0: chk(10)
                    r_gc = load_slab(base + 8)
                    q_proj(r_q, 0, 0)

                    def dbias(kb, st, qt):
                        d_ = kb - 4 * qt
                        if -1 <= d_ <= 4:
                            return ("G", 0, 512 - 128 * d_)
                        return ("C", cst[:, C_CL + c:C_CL + c + 1] if d_ < 0 else cst[:, C_CR + c:C_CR + c + 1])
                    def diff_order(qt):
                        near = [kb for kb in range(16) if -1 <= kb - 4 * qt <= 4]
                        far = [kb for kb in range(16) if kb not in near]
                        out = []
                        while near or far:
                            if far: out.append(far.pop(0))
                            if near: out.append(near.pop(0))
                        return out
                    for qt in range(4):
                        jb = qt % 2
                        if qt < 3: q_proj(r_q, qt + 1, (qt + 1) % 2)
                        sg_r = gate_for(r_gc, qt)
                        X = 4; Y = 5
                        attention([half_stream(0, jb, [(X, lambda kb: Vt[:, kb, 0:128]), (Y, lambda kb: ones[:])])], diff_order(qt), qt, 0.125,
                                  lambda kb, st, qt=qt: dbias(kb, st, qt))
                        r1 = finish(X, Y, None, qt, "raw")
                        e = k.begin("dve", [("tmp", r1)], ["o1"])
                        ins = e.tensor_copy(out=o1[:], in_=tmp[:, r1, :]); k.end(ins)
                        X = 6; Y = 7
                        attention([half_stream(64, jb, [(X, lambda kb: Vt[:, kb, 0:128]), (Y, lambda kb: ones[:])])], diff_order(qt), qt, 0.125,
                                  lambda kb, st, qt=qt: dbias(kb, st, qt))
                        r2 = finish(X, Y, None, qt, "raw")
                        e = k.begin("dve", [("tmp", r2), "o1", "neglam"], ["o1"])
                        ins = e.scalar_tensor_tensor(out=o1[:], in0=tmp[:, r2, :], scalar=misc[:, 4:5], in1=o1[:],
                                                     op0=ALU.mult, op1=ALU.add); k.end(ins)
                        r = nxt("sq", 2)
                        e = k.begin("act", ["o1"], [("sq", r)])
                        ins = e.activation(out=sq[:, r, :], in_=o1[:], func=AF.Square, scale=1.0 / math.sqrt(128.0)); k.end(ins)
                        mb = gen()
                        mm_group(ps[mb][:], ("ps", mb), [(ones[:], sq[:, r, :])], [("sq", r), "ones"])
                        rstd_from(mb, rstd[:], "rstd")
                        e = k.begin("dve", ["o1", "rstd", "cst"], ["o1"])
                        ins = e.scalar_tensor_tensor(out=o1[:], in0=o1[:], scalar=cst[:, C_SUBLN:C_SUBLN + 1], in1=rstd[:],
                                                     op0=ALU.mult, op1=ALU.mult); k.end(ins)
                        e = k.begin("dve", ["o1", ("sgt", sg_r)], ["o1"])
                        ins = e.tensor_tensor(out=o1[:], in0=o1[:], in1=sgt[:, sg_r, :], op=ALU.mult); k.end(ins)
                        e = k.begin("dve", ["o1", ("macc", qt)], [("macc", qt)])
                        ins = e.scalar_tensor_tensor(out=macc[:, tsl(qt)], in0=o1[:], scalar=1.0 - lam_init, in1=macc[:, tsl(qt)],
                                                     op0=ALU.mult, op1=ALU.add); k.end(ins)

                    if c == 0: chk(11)
                    k.mark('wout%d' % c)
                    wb = nxt("wout", 2)
                    k.dma("pool", ("wout", wb), [], [("wout", wb)], [(wout[:, wb, :], wout_d[l, c * 128:(c + 1) * 128, :])])
                    for tt in range(4):
                        r = 0
                        e = k.begin("act", [("macc", tt)], [("mt", r)])
                        ins = e.activation(out=mt[:, r, :], in_=macc[:, tsl(tt)], func=AF.Copy); k.end(ins)
                        for dc in range(8):
                            ob = gen()
                            mm_group(ps[ob][:], ("ps", ob), [(wout[:, wb, dc * 128:(dc + 1) * 128], mt[:, r, :])],
                                     [("wout", wb), ("mt", r)])
                            e = k.begin("dve", [("ps", ob), ("hT", dc, tt)], [("hT", dc, tt)])
                            ins = e.tensor_tensor(out=hT[:, dc, tsl(tt)], in0=ps[ob][:], in1=hT[:, dc, tsl(tt)], op=ALU.add); k.end(ins)
                k.barrier()

        out_keys = []
        for s in range(nseq):
            k.dma("sp", "xin", [], [("hT", kc, tt) for kc in range(8) for tt in range(4)],
                  [(hT[:, kc, :], xT[s, kc * 128:(kc + 1) * 128, :]) for kc in range(8)])
            for l in range(depth):
                k.dma("sp", "cst", [], ["cst"], [(cst[:], cst_d[l])])
                if "ffn1" in phases: ffn(l, 0, C_FFN1)
                if "mix" in phases: mixer(l)
                if "ffn2" in phases: ffn(l, 1, C_FFN2)
                if "ple" in phases: ple(l, s)
            with ExitStack() as ph:
                ot = sb(ph, "ot", [128, 2, 512], F32)
                rot["ot"] = 0

                def fo(kc, tt):
                    r = nxt("ot", 2)
                    e = k.begin("dve", [("hT", kc, tt), "rstd", "cst"], [("ot", r)])
                    ins = e.scalar_tensor_tensor(out=ot[:, r, :], in0=hT[:, kc, tsl(tt)],
                                                 scalar=cst[:, C_FIN + kc:C_FIN + kc + 1], in1=rstd[:],
                                                 op0=ALU.mult, op1=ALU.mult); k.end(ins)
                    k.dma("sp", ("ot", r), [("ot", r)], [], [(yT[s, kc * 128:(kc + 1) * 128, tsl(tt)], ot[:, r, :])])
                    if ("ot", r) not in out_keys: out_keys.append(("ot", r))
                norm(C_FIN, fo)
                k.barrier()
        if "dbg" in k.dsem: out_keys.append("dbg")
        for key in out_keys:
            nc.sync.wait_ge(k.dsem[key], k.dcnt[key])
    LAST_MARKS[:] = k.marks + [('end', k.nmm)]
    return nc


_NC_CACHE = {}


def kernel(**inputs):
    inp = {kk: np.asarray(v, dtype=np.float32) for kk, v in inputs.items()}
    shared = _prep_shared(inp)
    xT = np.ascontiguousarray(inp["x"].transpose(0, 2, 1))
    pT = np.ascontiguousarray(inp["p"].transpose(0, 1, 3, 2))
    if "nc" not in _NC_CACHE:
        _NC_CACHE["nc"] = build()
    nc = _NC_CACHE["nc"]
    in_maps = []
    for c in range(NCORE):
        m = dict(shared)
        m["xT"] = xT[c * NSEQ:(c + 1) * NSEQ]
        m["pT"] = np.ascontiguousarray(pT[:, c * NSEQ:(c + 1) * NSEQ])
        in_maps.append(m)
    res = run_bass_kernel_spmd(nc, in_maps, core_ids=list(range(NCORE)))
    yT = np.concatenate([r["yT"] for r in res.results], axis=0)
    return np.ascontiguousarray(yT.transpose(0, 2, 1)).astype(np.float32)
```

```python
import math
from contextlib import ExitStack
import numpy as np
import concourse.bass as bass
import concourse.mybir as mybir
from concourse.bass_utils import run_bass_kernel_spmd

F32 = mybir.dt.float32
BF16 = mybir.dt.bfloat16
AF = mybir.ActivationFunctionType
ALU = mybir.AluOpType
AX = mybir.AxisListType

D = 1024; S = 2048; L = 4; B = 32; FF = 2816; PLE = 256
NCORE = 8; NSEQ = B // NCORE
EPS = 1e-6
NSLAB = 79
NCONST = 334
C_FFN1, C_MIX, C_FFN2, C_PLE, C_FIN, C_QN, C_KVN, C_SUBLN, C_SINK, C_CL, C_CR, C_LQ1, C_LK1, C_LQ2, C_LK2, C_SINK2 = \
    0, 8, 16, 24, 32, 40, 43, 45, 46, 54, 62, 70, 134, 198, 262, 326
GW = 1152
MIXSTOP = None
LAST_MARKS = []


class _Stop(Exception):
    pass


def _t5_bucket(rel):
    nb = 16; max_exact = 8
    try:
        import jax, jax.numpy as jnp
        with jax.default_device(jax.devices("cpu")[0]):
            r = jnp.asarray(rel.astype(np.int32))
            base = jnp.where(r > 0, nb, 0)
            n = jnp.abs(r)
            nf = jnp.maximum(n, 1).astype(jnp.float32)
            large = max_exact + (jnp.log(nf / max_exact) / math.log(128 / max_exact) * (nb - max_exact)).astype(jnp.int32)
            large = jnp.minimum(large, nb - 1)
            return np.asarray(base + jnp.where(n < max_exact, n, large))
    except Exception:
        base = np.where(rel > 0, nb, 0)
        n = np.abs(rel)
        nf = np.maximum(n, 1).astype(np.float32)
        large = max_exact + (np.log(nf / np.float32(max_exact)) / np.float32(math.log(128 / max_exact))
                             * np.float32(nb - max_exact)).astype(np.int32)
        large = np.minimum(large, nb - 1)
        return base + np.where(n < max_exact, n, large)


def _win_cols():
    cols = []
    ar = np.arange(128)
    for i in range(3): cols.append(ar + 128 * i)
    for i in range(2): cols.append(384 + ar + 128 * i)
    pad = np.zeros(32, np.int64)
    cols.append(np.concatenate([576 + np.arange(96), pad]))
    cols.append(np.concatenate([576 + np.arange(64), 656 + np.arange(16), 640 + np.arange(16), pad]))
    a64 = np.arange(64)
    for c in range(8):
        kv = c // 2
        cols.append(672 + 128 * c + ar)
        cols.append(np.concatenate([1696 + 64 * kv + a64] * 2))
        cols.append(np.concatenate([1952 + 64 * kv + a64] * 2))
        cols.append(2208 + 128 * c + ar)
        cols.append(3232 + 128 * c + ar)
        cols.append(4256 + 128 * c + ar)
        cols.append(5280 + 128 * c + ar)
        cols.append(6304 + 128 * c + ar)
        cols.append(7328 + 128 * c + ar)
    return np.stack(cols)


def _prep_shared(inp):
    f = np.float32
    out = {}
    cols = _win_cols().reshape(-1)
    w_in = inp["w_in"]
    wi = np.empty((L, NSLAB, 128, 8 * 128), f)
    for l in range(L):
        g = w_in[l][:, cols].reshape(8, 128, NSLAB, 128)
        wi[l] = g.transpose(2, 1, 0, 3).reshape(NSLAB, 128, 1024)
    out["wi"] = wi
    uq = inp["mla_w_uq"]
    ucols = []
    for c in range(8):
        nrm = []; swp = []
        for h in (2 * c, 2 * c + 1):
            nrm.append(96 * h + np.arange(96))
            swp.append(np.concatenate([96 * h + np.arange(64), 96 * h + 80 + np.arange(16), 96 * h + 64 + np.arange(16)]))
        ucols.append(np.concatenate(nrm + swp))
    ucols = np.stack(ucols)
    wuq = np.empty((L, 8, 128, 3 * 384), f)
    for l in range(L):
        g = uq[l][:, ucols.reshape(-1)].reshape(3, 128, 8, 384)
        wuq[l] = g.transpose(2, 1, 0, 3).reshape(8, 128, 3 * 384)
    out["wuq"] = wuq
    ukv = inp["mla_w_ukv"]
    out["wukv"] = np.ascontiguousarray(
        ukv.reshape(L, 2, 128, 8, 256).transpose(0, 3, 2, 1, 4).reshape(L, 8, 128, 512))
    wgu = np.empty((L, 2, 22, 128, 2048), f)
    wd = np.empty((L, 2, 2, 8, 128, 11 * 128), f)
    for w, (gn, un, dn) in enumerate((("ffn1_w_gate", "ffn1_w_up", "ffn1_w_down"),
                                      ("ffn2_w_gate", "ffn2_w_up", "ffn2_w_down"))):
        g = inp[gn].reshape(L, 8, 128, 22, 128).transpose(0, 3, 2, 1, 4)
        u = inp[un].reshape(L, 8, 128, 22, 128).transpose(0, 3, 2, 1, 4)
        wgu[:, w] = np.stack([g, u], axis=3).reshape(L, 22, 128, 2048)
        dd = inp[dn].reshape(L, 2, 11, 128, 8, 128).transpose(0, 1, 4, 3, 2, 5)
        wd[:, w] = dd.reshape(L, 2, 8, 128, 11 * 128)
    out["wgu"] = wgu
    out["wd"] = wd
    out["wout"] = np.ascontiguousarray(inp["w_out"])
    out["wpg"] = np.ascontiguousarray(
        inp["ple_w_gate"].reshape(L, 8, 128, 8, 128).transpose(0, 3, 2, 1, 4).reshape(L, 8, 128, 1024))
    out["wpp"] = np.ascontiguousarray(inp["ple_w_proj"])
    cst = np.zeros((L, 128, NCONST), f)
    for l in range(L):
        for nm, o in (("ffn1_norm", C_FFN1), ("mix_norm", C_MIX), ("ffn2_norm", C_FFN2), ("ple_norm", C_PLE)):
            cst[l, :, o:o + 8] = inp[nm][l].reshape(8, 128).T
        cst[l, :, C_FIN:C_FIN + 8] = inp["final_norm"].reshape(8, 128).T
        cst[l, :, C_QN:C_QN + 3] = inp["mla_q_norm"][l].reshape(3, 128).T
        cst[l, :, C_KVN:C_KVN + 2] = inp["mla_kv_norm"][l].reshape(2, 128).T
        cst[l, :, C_SUBLN] = inp["diff_subln"][l]
        sk = inp["swa_sink"][l]
        for c in range(8):
            cst[l, :64, C_SINK + c] = sk[2 * c]
            cst[l, 64:, C_SINK + c] = sk[2 * c + 1]
            cst[l, :64, C_SINK2 + c] = sk[2 * c + 1]
            cst[l, 64:, C_SINK2 + c] = sk[2 * c]
            cst[l, :, C_CL + c] = inp["rel_table"][15, 16 + c]
            cst[l, :, C_CR + c] = inp["rel_table"][31, 16 + c]
        for nm, o in (("diff_lambda_q1", C_LQ1), ("diff_lambda_k1", C_LK1), ("diff_lambda_q2", C_LQ2), ("diff_lambda_k2", C_LK2)):
            cst[l, :, o:o + 64] = inp[nm][l][None, :]
    out["cst"] = cst
    rel = np.arange(128)[:, None] - np.arange(GW)[None, :] + 512
    bk = _t5_bucket(rel)
    gm = np.ascontiguousarray(inp["rel_table"][bk].transpose(2, 0, 1)).astype(f)
    gm[:16][:, np.abs(rel) > 128] = -30000.0
    out["gm"] = gm
    inv = (10000.0 ** (-np.arange(0, 32, 2, dtype=np.float32) / np.float32(32))).astype(f)
    ang = (np.arange(S, dtype=f)[:, None] * inv[None, :]).astype(f)
    cs, sn = np.cos(ang).astype(f).T, np.sin(ang).astype(f).T
    tab = np.zeros((128, 2, S), f)
    tab[64:96, 0] = np.concatenate([cs, cs], 0)
    tab[64:96, 1] = np.concatenate([-sn, sn], 0)
    out["tab"] = tab
    return out


class KB:
    def __init__(self, nc, es):
        self.nc = nc; self.es = es
        self.eng = {"pe": nc.tensor, "act": nc.scalar, "dve": nc.vector, "pool": nc.gpsimd, "sp": nc.sync}
        self.sem = {e: es.enter_context(nc.semaphore("s_" + e)) for e in self.eng}
        self.cnt = {e: 0 for e in self.eng}
        self.known = {e: {} for e in self.eng}
        self.lastw = {}; self.readers = {}
        self.dsem = {}; self.dcnt = {}
        self._cur = None
        self.nmm = 0; self.marks = []

    def _deps(self, eng, reads, writes, is_dma):
        need = {}

        def add(rec, raw):
            if rec[0] == "c":
                e2 = rec[1]
                if e2 == eng and not is_dma:
                    if eng == "pe":
                        return
            k = rec[:2]
            if need.get(k, 0) < rec[2]:
                need[k] = rec[2]
        for r in reads:
            w = self.lastw.get(r)
            if w: add(w, True)
            if type(r) is tuple and r[0] == "ps":
                rd = self.readers.get(r)
                if rd:
                    for rec in rd.values(): add(rec, False)
        for k in writes:
            w = self.lastw.get(k)
            if w: add(w, False)
            rd = self.readers.get(k)
            if rd:
                for rec in rd.values(): add(rec, False)
        kn = self.known[eng]
        e = self.eng[eng]
        for sk, v in need.items():
            if kn.get(sk, 0) < v:
                sem = self.sem[sk[1]] if sk[0] == "c" else self.dsem[sk[1]]
                e.wait_ge(sem, v)
                kn[sk] = v

    def _commit(self, rec, reads, writes):
        for r in reads:
            self.readers.setdefault(r, {})[rec[:2]] = rec
        for k in writes:
            self.lastw[k] = rec; self.readers[k] = {}

    def begin(self, eng, reads, writes):
        self._deps(eng, reads, writes, False)
        self._cur = (eng, reads, writes)
        return self.eng[eng]

    def end(self, ins):
        eng, reads, writes = self._cur
        self.cnt[eng] += 1
        ins.then_inc(self.sem[eng], 1)
        self._commit(("c", eng, self.cnt[eng]), reads, writes)

    def dma(self, q, key, reads, writes, pairs):
        if key not in self.dsem:
            self.dsem[key] = self.es.enter_context(self.nc.semaphore("d_" + str(len(self.dsem))))
            self.dcnt[key] = 0
        self._deps(q, reads, writes, True)
        e = self.eng[q]
        for o, i in pairs:
            e.dma_start(out=o, in_=i).then_inc(self.dsem[key], 16)
            self.dcnt[key] += 16
        self._commit(("d", key, self.dcnt[key]), reads, writes)

    def mm(self, e, *a, **kw):
        self.nmm += 1
        return e.matmul(*a, **kw)

    def mark(self, name):
        self.marks.append((name, self.nmm))

    def barrier(self):
        for e in self.eng:
            kn = self.known[e]
            for e2 in self.eng:
                if e2 == "pe" and e == "pe": continue
                if self.cnt[e2] > kn.get(("c", e2), 0):
                    self.eng[e].wait_ge(self.sem[e2], self.cnt[e2]); kn[("c", e2)] = self.cnt[e2]
            for key, v in self.dcnt.items():
                if v > kn.get(("d", key), 0):
                    self.eng[e].wait_ge(self.dsem[key], v); kn[("d", key)] = v
        self.lastw.clear(); self.readers.clear()


def build(nseq=NSEQ, depth=L, phases=("ffn1", "mix", "ffn2", "ple"), fin=True):
    nc = bass.Bass("TRN2", target_bir_lowering=False)

    def din(name, shape):
        return nc.dram_tensor(name, shape, F32, kind="ExternalInput").ap()
    xT = din("xT", [nseq, D, S]); pT = din("pT", [L, nseq, PLE, S])
    wi = din("wi", [L, NSLAB, 128, 1024]); wuq_d = din("wuq", [L, 8, 128, 1152]); wukv_d = din("wukv", [L, 8, 128, 512])
    wgu_d = din("wgu", [L, 2, 22, 128, 2048]); wd_d = din("wd", [L, 2, 2, 8, 128, 1408])
    wout_d = din("wout", [L, D, D]); wpg_d = din("wpg", [L, 8, 128, 1024]); wpp_d = din("wpp", [L, PLE, D])
    cst_d = din("cst", [L, 128, NCONST]); gm_d = din("gm", [24, 128, GW]); tab_d = din("tab", [128, 2, S])
    yT = nc.dram_tensor("yT", [nseq, D, S], F32, kind="ExternalOutput").ap()
    dbg = nc.dram_tensor("dbg", [128, S], F32, kind="ExternalOutput").ap() if MIXSTOP is not None else None
    dbgref = {}

    with ExitStack() as es:
        k = KB(nc, es)

        uid = [0]

        def sb(st, name, shape, dt):
            uid[0] += 1
            return st.enter_context(nc.sbuf_tensor("t%d_%s" % (uid[0], name), shape, dt))
        hT = sb(es, "hT", [128, 8, S], F32)
        uT = sb(es, "uT", [128, 8, S], BF16)
        cst = sb(es, "cst", [128, NCONST], F32)
        ones = sb(es, "ones", [128, 128], BF16)
        onesA = sb(es, "onesA", [128, 128], BF16)
        onesB = sb(es, "onesB", [128, 128], BF16)
        sq = sb(es, "sq", [128, 2, 512], BF16)
        rstd = sb(es, "rstd", [128, 512], F32)
        sgt = sb(es, "sgt", [128, 2, 512], F32)
        tmp = sb(es, "tmp", [128, 2, 512], F32)
        rs = sb(es, "rs", [128, 2, 512], F32)
        misc = sb(es, "misc", [128, 80], F32)
        ps = [es.enter_context(nc.psum_tensor("ps%d" % i, [128, 512], F32)) for i in range(8)]
        rot = {"gen": 0, "acc": 0, "sq": 0, "sgt": 0, "tmp": 0, "rs": 0}

        def gen():
            rot["gen"] = (rot["gen"] + 1) % 4; return rot["gen"]

        def acc():
            rot["acc"] = (rot["acc"] + 1) % 4; return 4 + rot["acc"]

        def nxt(name, n):
            rot[name] = (rot[name] + 1) % n; return rot[name]

        def tsl(tt):
            return slice(tt * 512, (tt + 1) * 512)

        e = k.begin("dve", [], ["ones"]); i_ = e.memset(ones[:], 1.0); k.end(i_)
        e = k.begin("dve", [], ["onesA"]); i_ = e.memset(onesA[:, 0:64], 1.0); k.end(i_)
        e = k.begin("dve", [], ["onesA2"]); i_ = e.memset(onesA[:, 64:128], 0.0); k.end(i_)
        e = k.begin("dve", [], ["onesB"]); i_ = e.memset(onesB[:, 64:128], 1.0); k.end(i_)
        e = k.begin("dve", [], ["onesB2"]); i_ = e.memset(onesB[:, 0:64], 0.0); k.end(i_)

        def mm_group(out_ap, pskey, pairs, reads):
            e = k.begin("pe", reads, [pskey])
            n = len(pairs)
            for i, (lh, rh) in enumerate(pairs):
                ins = k.mm(e, out_ap, lh, rh, start=(i == 0), stop=(i == n - 1))
            k.end(ins)

        def rstd_from(psb, dst, dkey):
            e = k.begin("act", [("ps", psb), "misc"], [dkey])
            ins = e.activation(out=dst, in_=ps[psb][:], func=AF.Ln, bias=misc[:, 8:9], scale=1.0); k.end(ins)
            e = k.begin("act", [dkey], [dkey])
            ins = e.activation(out=dst, in_=dst, func=AF.Exp, scale=-0.5); k.end(ins)

        e = k.begin("dve", [], ["misc"]); e.memset(misc[:, 0:8], 0.0); e.memset(misc[:, 9:10], 0.0); e.memset(misc[:, 11:80], 0.0); e.memset(misc[:, 10:11], 1.0); i_ = e.memset(misc[:, 8:9], EPS); k.end(i_)

        k.barrier()

        def norm(gcol, out_fn):
            for tt in range(4):
                mb = acc()
                for kc in range(8):
                    r = nxt("sq", 2)
                    e = k.begin("act", [("hT", kc, tt)], [("sq", r)])
                    ins = e.activation(out=sq[:, r, :], in_=hT[:, kc, tsl(tt)], func=AF.Square, scale=1.0 / 32.0); k.end(ins)
                    e = k.begin("pe", [("sq", r), "ones"], [("ps", mb)])
                    ins = k.mm(e, ps[mb][:], ones[:], sq[:, r, :], start=(kc == 0), stop=(kc == 7)); k.end(ins)
                rstd_from(mb, rstd[:], "rstd")
                for kc in range(8):
                    out_fn(kc, tt)

        def norm_to_u(gcol):
            def f(kc, tt):
                e = k.begin("dve", [("hT", kc, tt), "rstd", "cst"], [("uT", tt)])
                ins = e.scalar_tensor_tensor(out=uT[:, kc, tsl(tt)], in0=hT[:, kc, tsl(tt)],
                                             scalar=cst[:, gcol + kc:gcol + kc + 1], in1=rstd[:],
                                             op0=ALU.mult, op1=ALU.mult); k.end(ins)
            norm(gcol, f)

        uT_all = [("uT", t) for t in range(4)]

        def ffn(l, which, gcol):
            k.mark('ffn%d' % which)
            with ExitStack() as ph:
                aT = sb(ph, "aT", [128, 11, S], BF16)
                wgu = sb(ph, "wgu", [128, 2, 2048], BF16)
                wd = sb(ph, "wd", [128, 2, 1408], BF16)
                norm_to_u(gcol)
                for half in range(2):
                    for j in range(11):
                        fc = half * 11 + j; bf = fc % 2
                        k.dma("pool", ("wgu", bf), [], [("wgu", bf)], [(wgu[:, bf, :], wgu_d[l, which, fc])])
                        for tt in range(4):
                            gb = gen(); ub = acc()
                            mm_group(ps[gb][:], ("ps", gb),
                                     [(wgu[:, bf, kc * 128:(kc + 1) * 128], uT[:, kc, tsl(tt)]) for kc in range(8)],
                                     [("wgu", bf), ("uT", tt)])
                            mm_group(ps[ub][:], ("ps", ub),
                                     [(wgu[:, bf, 1024 + kc * 128:1024 + (kc + 1) * 128], uT[:, kc, tsl(tt)]) for kc in range(8)],
                                     [("wgu", bf), ("uT", tt)])
                            r = nxt("sgt", 2)
                            e = k.begin("act", [("ps", gb)], [("sgt", r)])
                            ins = e.activation(out=sgt[:, r, :], in_=ps[gb][:], func=AF.Silu); k.end(ins)
                            e = k.begin("dve", [("ps", ub), ("sgt", r)], [("aT", j, tt)])
                            ins = e.tensor_tensor(out=aT[:, j, tsl(tt)], in0=ps[ub][:], in1=sgt[:, r, :], op=ALU.mult); k.end(ins)
                    for dc in range(8):
                        bf = dc % 2
                        k.dma("pool", ("wd", bf), [], [("wd", bf)], [(wd[:, bf, :], wd_d[l, which, half, dc])])
                        for tt in range(4):
                            ob = gen()
                            mm_group(ps[ob][:], ("ps", ob),
                                     [(wd[:, bf, j * 128:(j + 1) * 128], aT[:, j, tsl(tt)]) for j in range(11)],
                                     [("wd", bf)] + [("aT", j, tt) for j in range(11)])
                            e = k.begin("dve", [("ps", ob), ("hT", dc, tt)], [("hT", dc, tt)])
                            ins = e.scalar_tensor_tensor(out=hT[:, dc, tsl(tt)], in0=ps[ob][:], scalar=0.5,
                                                         in1=hT[:, dc, tsl(tt)], op0=ALU.mult, op1=ALU.add); k.end(ins)
                k.barrier()

        def ple(l, s):
            k.mark('ple')
            with ExitStack() as ph:
                pTt = sb(ph, "pTt", [128, 2, S], BF16)
                wpp = sb(ph, "wpp", [128, 2, D], BF16)
                wpg = sb(ph, "wpg", [128, 2, 1024], BF16)
                k.dma("pool", "pTt", [], ["pTt"], [(pTt[:, :, :], pT[l, s].rearrange("(c p) t -> p c t", p=128))])
                k.dma("pool", "wpp", [], ["wpp"], [(wpp[:, :, :], wpp_d[l].rearrange("(c p) n -> p c n", p=128))])
                norm_to_u(C_PLE)
                for dc in range(8):
                    bf = dc % 2
                    k.dma("pool", ("wpg", bf), [], [("wpg", bf)], [(wpg[:, bf, :], wpg_d[l, dc])])
                    for tt in range(4):
                        gb = gen(); pb = acc()
                        mm_group(ps[gb][:], ("ps", gb),
                                 [(wpg[:, bf, kc * 128:(kc + 1) * 128], uT[:, kc, tsl(tt)]) for kc in range(8)],
                                 [("wpg", bf), ("uT", tt)])
                        mm_group(ps[pb][:], ("ps", pb),
                                 [(wpp[:, kc, dc * 128:(dc + 1) * 128], pTt[:, kc, tsl(tt)]) for kc in range(2)],
                                 ["wpp", "pTt"])
                        r = sigmoid_gate(gb)
                        r2 = nxt("tmp", 2)
                        e = k.begin("dve", [("ps", pb), ("sgt", r)], [("tmp", r2)])
                        ins = e.tensor_tensor(out=tmp[:, r2, :], in0=ps[pb][:], in1=sgt[:, r, :], op=ALU.mult); k.end(ins)
                        e = k.begin("dve", [("tmp", r2), ("hT", dc, tt)], [("hT", dc, tt)])
                        ins = e.tensor_tensor(out=hT[:, dc, tsl(tt)], in0=tmp[:, r2, :], in1=hT[:, dc, tsl(tt)], op=ALU.add); k.end(ins)
                k.barrier()

        def sigmoid_gate(gb):
            r = nxt("sgt", 2)
            e = k.begin("act", [("ps", gb)], [("sgt", r)])
            ins = e.activation(out=sgt[:, r, :], in_=ps[gb][:], func=AF.Exp, scale=-1.0); k.end(ins)
            e = k.begin("act", [("sgt", r), "misc"], [("sgt", r)])
            ins = e.activation(out=sgt[:, r, :], in_=sgt[:, r, :], func=AF.Ln, bias=misc[:, 10:11], scale=1.0); k.end(ins)
            e = k.begin("act", [("sgt", r)], [("sgt", r)])
            ins = e.activation(out=sgt[:, r, :], in_=sgt[:, r, :], func=AF.Exp, scale=-1.0); k.end(ins)
            return r

        def mixer(l):
            k.mark('mixer')
            ph = ExitStack()
            try:
                _mixer(l, ph)
            except _Stop:
                if dbg is not None and "macc" in dbgref:
                    k.dma("sp", "dbg", [("macc", t) for t in range(4)], [], [(dbg[:, :], dbgref["macc"][:, :])])
            k.barrier()
            ph.close()

        def _mixer(l, ph):
            lam_init = 0.8 - 0.6 * math.exp(-0.3 * l)
            if True:
                cqn = sb(ph, "cqn", [128, 3, S], BF16)
                ckvn = sb(ph, "ckvn", [128, 2, S], BF16)
                kr = sb(ph, "kr", [128, S], BF16)
                tab = sb(ph, "tab", [128, 2, S], BF16)
                kA = sb(ph, "kA", [128, S], BF16)
                kBt = sb(ph, "kBt", [128, S], BF16)
                qJ = sb(ph, "qJ", [128, 2, 2, 512], BF16)
                Vt = sb(ph, "Vt", [128, 16, 192], BF16)
                G = sb(ph, "G", [128, 2, GW], F32)
                PT = sb(ph, "PT", [128, 4, 512], BF16)
                macc = sb(ph, "macc", [128, S], F32)
                dbgref["macc"] = macc
                o1 = sb(ph, "o1", [128, 512], F32)
                mt = sb(ph, "mt", [128, 1, 512], BF16)
                wsl = sb(ph, "wsl", [128, 4, 1024], BF16)
                wuq = sb(ph, "wuq", [128, 3, 384], BF16)
                wukv = sb(ph, "wukv", [128, 2, 2, 128], BF16)
                wout = sb(ph, "wout", [128, 2, D], BF16)
                rot.update({"wsl": 0, "pt": 0, "mt": 0, "wout": 0, "qj": 0})

                def chk(n):
                    if MIXSTOP == n:
                        raise _Stop()

                k.dma("pool", "tab", [], ["tab"], [(tab[:, :, :], tab_d[:, :, :])])
                norm_to_u(C_MIX)

                e = k.begin("dve", ["cst"], ["lamt"])
                ins = e.tensor_tensor(out=tmp[:, 0, 0:64], in0=cst[:, C_LQ1:C_LQ1 + 64], in1=cst[:, C_LK1:C_LK1 + 64], op=ALU.mult)
                ins = e.tensor_tensor(out=tmp[:, 0, 64:128], in0=cst[:, C_LQ2:C_LQ2 + 64], in1=cst[:, C_LK2:C_LK2 + 64], op=ALU.mult)
                k.end(ins)
                e = k.begin("dve", ["lamt"], ["lam1"])
                ins = e.reduce_sum(out=misc[:, 0:1], in_=tmp[:, 0, 0:64], axis=AX.X)
                ins = e.reduce_sum(out=misc[:, 1:2], in_=tmp[:, 0, 64:128], axis=AX.X); k.end(ins)
                e = k.begin("act", ["lam1"], ["lam2"])
                ins = e.activation(out=misc[:, 2:4], in_=misc[:, 0:2], func=AF.Exp); k.end(ins)
                e = k.begin("dve", ["lam2"], ["lam3"])
                ins = e.tensor_tensor(out=misc[:, 4:5], in0=misc[:, 3:4], in1=misc[:, 2:3], op=ALU.subtract); k.end(ins)
                e = k.begin("dve", ["lam3"], ["neglam"])
                ins = e.tensor_scalar(out=misc[:, 4:5], in0=misc[:, 4:5], scalar1=-lam_init, scalar2=None, op0=ALU.add); k.end(ins)
                e = k.begin("act", ["cst"], ["esk"])
                e.activation(out=misc[:, 24:32], in_=cst[:, C_SINK2:C_SINK2 + 8], func=AF.Exp)
                ins = e.activation(out=misc[:, 16:24], in_=cst[:, C_SINK:C_SINK + 8], func=AF.Exp); k.end(ins)
                chk(1)

                def load_slab(idx):
                    r = nxt("wsl", 4)
                    k.dma("pool", ("wsl", r), [], [("wsl", r)], [(wsl[:, r, :], wi[l, idx])])
                    return r

                def slab_mm(psb, r, tt, m=128):
                    mm_group(ps[psb][0:m, :], ("ps", psb),
                             [(wsl[:, r, kc * 128:kc * 128 + m], uT[:, kc, tsl(tt)]) for kc in range(8)],
                             [("wsl", r), ("uT", tt)])

                def latent(slab0, n, gcol, dst, dname):
                    rr = [load_slab(slab0 + i) for i in range(n)]
                    for tt in range(4):
                        bs = [gen() for _ in range(n)]
                        mb = acc()
                        for i in range(n):
                            slab_mm(bs[i], rr[i], tt)
                        for i in range(n):
                            r = nxt("sq", 2)
                            e = k.begin("act", [("ps", bs[i])], [("sq", r)])
                            ins = e.activation(out=sq[:, r, :], in_=ps[bs[i]][:], func=AF.Square,
                                               scale=1.0 / math.sqrt(128.0 * n)); k.end(ins)
                            e = k.begin("pe", [("sq", r), "ones"], [("ps", mb)])
                            ins = k.mm(e, ps[mb][:], ones[:], sq[:, r, :], start=(i == 0), stop=(i == n - 1)); k.end(ins)
                        rstd_from(mb, rstd[:], "rstd")
                        for i in range(n):
                            e = k.begin("dve", [("ps", bs[i]), "rstd", "cst"], [(dname, tt)])
                            ins = e.scalar_tensor_tensor(out=dst[:, i, tsl(tt)], in0=ps[bs[i]][:],
                                                         scalar=cst[:, gcol + i:gcol + i + 1], in1=rstd[:],
                                                         op0=ALU.mult, op1=ALU.mult); k.end(ins)
                k.mark('latent')
                latent(0, 3, C_QN, cqn, "cqn")
                latent(3, 2, C_KVN, ckvn, "ckvn")
                chk(2)

                def rope_rows(pa, pb_, out_ap, okey, tt):
                    r1 = nxt("tmp", 2)
                    e = k.begin("dve", [("ps", pa), "tab"], [("tmp", r1)])
                    ins = e.tensor_tensor(out=tmp[64:96, r1, :], in0=ps[pa][64:96, :], in1=tab[64:96, 0, tsl(tt)], op=ALU.mult); k.end(ins)
                    r2 = nxt("tmp", 2)
                    e = k.begin("dve", [("ps", pb_), "tab"], [("tmp", r2)])
                    ins = e.tensor_tensor(out=tmp[64:96, r2, :], in0=ps[pb_][64:96, :], in1=tab[64:96, 1, tsl(tt)], op=ALU.mult); k.end(ins)
                    e = k.begin("dve", [("tmp", r1), ("tmp", r2)], [okey])
                    ins = e.tensor_tensor(out=out_ap, in0=tmp[64:96, r1, :], in1=tmp[64:96, r2, :], op=ALU.add); k.end(ins)

                r_kr = load_slab(5); r_krs = load_slab(6)
                for tt in range(4):
                    pa = gen(); pb_ = gen()
                    slab_mm(pa, r_kr, tt, 96); slab_mm(pb_, r_krs, tt, 96)
                    rope_rows(pa, pb_, kr[64:96, tsl(tt)], ("kr", tt), tt)
                kr_all = [("kr", t) for t in range(4)]
                chk(3)

                def attention(streams, kbs, qt, scale, bias_fn):
                    items = [(kb, st) for kb in kbs for st in streams]
                    n = len(items)
                    LA = 3
                    pend = []
                    for i in range(n + LA):
                        if i < n:
                            kb, st = items[i]
                            sbk = gen()
                            mm_group(ps[sbk][:], ("ps", sbk), [(st["k"](kb), st["q"])], st["kreads"](kb) + st["qreads"])
                            p = nxt("pt", 4)
                            bmode = bias_fn(kb, st)
                            if bmode[0] == "G":
                                gi, c0 = bmode[1], bmode[2]
                                r = nxt("tmp", 2)
                                e = k.begin("dve", [("ps", sbk), ("G", gi)], [("tmp", r)])
                                ins = e.scalar_tensor_tensor(out=tmp[:, r, :], in0=ps[sbk][:], scalar=scale,
                                                             in1=G[:, gi, c0:c0 + 512], op0=ALU.mult, op1=ALU.add); k.end(ins)
                                e = k.begin("act", [("tmp", r)], [("pt", p)])
                                ins = e.activation(out=PT[:, p, :], in_=tmp[:, r, :], func=AF.Exp); k.end(ins)
                            elif bmode[0] == "C":
                                e = k.begin("act", [("ps", sbk), "cst"], [("pt", p)])
                                ins = e.activation(out=PT[:, p, :], in_=ps[sbk][:], func=AF.Exp, bias=bmode[1], scale=scale); k.end(ins)
                            else:
                                e = k.begin("act", [("ps", sbk)], [("pt", p)])
                                ins = e.activation(out=PT[:, p, :], in_=ps[sbk][:], func=AF.Exp, scale=scale); k.end(ins)
                            pend.append(p)
                        if i >= LA:
                            j = i - LA
                            kb, st = items[j]; p = pend[j]
                            first = (kb == kbs[0]); last = (kb == kbs[-1])
                            e = k.begin("pe", [("pt", p)] + st["vreads"](kb), [("ps", b_) for b_, _ in st["acc"]])
                            for b_, lf in st["acc"]:
                                ins = k.mm(e, ps[b_][:], lf(kb), PT[:, p, :], start=first, stop=last)
                            k.end(ins)

                def gate_for(rg, tt):
                    gb = gen()
                    slab_mm(gb, rg, tt)
                    return sigmoid_gate(gb)

                def finish(X, Y, sg_r, tt, mode, sinkcol=None):
                    r = nxt("rs", 2)
                    bcol = sinkcol if sinkcol is not None else 9
                    e = k.begin("act", [("ps", Y), "esk", "misc"], [("rs", r)])
                    ins = e.activation(out=rs[:, r, :], in_=ps[Y][:], func=AF.Ln, bias=misc[:, bcol:bcol + 1], scale=1.0); k.end(ins)
                    e = k.begin("act", [("rs", r)], [("rs", r)])
                    ins = e.activation(out=rs[:, r, :], in_=rs[:, r, :], func=AF.Exp, scale=-1.0); k.end(ins)
                    r2 = nxt("tmp", 2)
                    e = k.begin("dve", [("ps", X), ("rs", r)], [("tmp", r2)])
                    ins = e.tensor_tensor(out=tmp[:, r2, :], in0=ps[X][:], in1=rs[:, r, :], op=ALU.mult); k.end(ins)
                    if mode == "raw":
                        return r2
                    if mode == "set":
                        e = k.begin("dve", [("tmp", r2), ("sgt", sg_r)], [("macc", tt)])
                        ins = e.tensor_tensor(out=macc[:, tsl(tt)], in0=tmp[:, r2, :], in1=sgt[:, sg_r, :], op=ALU.mult); k.end(ins)
                    else:
                        e = k.begin("dve", [("tmp", r2), ("sgt", sg_r)], [("tmp", r2)])
                        ins = e.tensor_tensor(out=tmp[:, r2, :], in0=tmp[:, r2, :], in1=sgt[:, sg_r, :], op=ALU.mult); k.end(ins)
                        e = k.begin("dve", [("tmp", r2), ("macc", tt)], [("macc", tt)])
                        ins = e.tensor_tensor(out=macc[:, tsl(tt)], in0=tmp[:, r2, :], in1=macc[:, tsl(tt)], op=ALU.add); k.end(ins)
                    return r2

                def finish_pair(XA, XB, sg_r, tt, mode, sinkcol=None):
                    r = nxt("rs", 2)
                    bcol = sinkcol if sinkcol is not None else 9
                    e = k.begin("act", [("ps", XA), "esk", "misc"], [("rs", r)])
                    ins = e.activation(out=rs[64:128, r, :], in_=ps[XA][64:128, :], func=AF.Ln, bias=misc[64:128, bcol:bcol + 1], scale=1.0); k.end(ins)
                    e = k.begin("act", [("ps", XB), "esk", "misc"], [("rs", r)])
                    ins = e.activation(out=rs[0:64, r, :], in_=ps[XB][0:64, :], func=AF.Ln, bias=misc[0:64, bcol:bcol + 1], scale=1.0); k.end(ins)
                    e = k.begin("act", [("rs", r)], [("rs", r)])
                    ins = e.activation(out=rs[:, r, :], in_=rs[:, r, :], func=AF.Exp, scale=-1.0); k.end(ins)
                    r2 = nxt("tmp", 2)
                    e = k.begin("dve", [("ps", XA), ("rs", r)], [("tmp", r2)])
                    ins = e.tensor_tensor(out=tmp[0:64, r2, :], in0=ps[XA][0:64, :], in1=rs[64:128, r, :], op=ALU.mult); k.end(ins)
                    e = k.begin("dve", [("ps", XB), ("rs", r)], [("tmp", r2)])
                    ins = e.tensor_tensor(out=tmp[64:128, r2, :], in0=ps[XB][64:128, :], in1=rs[0:64, r, :], op=ALU.mult); k.end(ins)
                    if mode == "set":
                        e = k.begin("dve", [("tmp", r2), ("sgt", sg_r)], [("macc", tt)])
                        ins = e.tensor_tensor(out=macc[:, tsl(tt)], in0=tmp[:, r2, :], in1=sgt[:, sg_r, :], op=ALU.mult); k.end(ins)
                    else:
                        e = k.begin("dve", [("tmp", r2), ("sgt", sg_r)], [("tmp", r2)])
                        ins = e.tensor_tensor(out=tmp[:, r2, :], in0=tmp[:, r2, :], in1=sgt[:, sg_r, :], op=ALU.mult); k.end(ins)
                        e = k.begin("dve", [("tmp", r2), ("macc", tt)], [("macc", tt)])
                        ins = e.tensor_tensor(out=macc[:, tsl(tt)], in0=tmp[:, r2, :], in1=macc[:, tsl(tt)], op=ALU.add); k.end(ins)

                def v_evac(pb_, g, padded):
                    pv3 = ps[pb_][:, :].rearrange("p (a b) -> p a b", b=128)
                    if padded:
                        e = k.begin("dve", [("ps", pb_)], [("V", g)])
                        e.tensor_copy(out=Vt[:, g * 4:(g + 1) * 4, 0:64], in_=pv3[:, :, 0:64])
                        ins = e.tensor_copy(out=Vt[:, g * 4:(g + 1) * 4, 128:192], in_=pv3[:, :, 64:128]); k.end(ins)
                    else:
                        e = k.begin("dve", [("ps", pb_)], [("V", g)])
                        ins = e.tensor_copy(out=Vt[:, g * 4:(g + 1) * 4, 0:128], in_=pv3); k.end(ins)

                V_all = [("V", g) for g in range(4)]

                for c in range(8):
                    base = 7 + 9 * c
                    k.mark('mla%d' % c)
                    k.dma("pool", "wuq", [], ["wuq"], [(wuq[:, :, :], wuq_d[l, c].rearrange("p (a b) -> p a b", b=384))])
                    k.dma("pool", "wukv", [], ["wukv"], [(wukv[:, :, :, :], wukv_d[l, c].rearrange("p (a b c) -> p a b c", b=2, c=128))])
                    r_ga = load_slab(base + 6)
                    e = k.begin("dve", [], V_all)
                    ins = e.memset(Vt[:, :, 64:128], 1.0); k.end(ins)
                    e = k.begin("dve", [], ["kAr", "kBr"] + [("kA", t) for t in range(4)] + [("kB", t) for t in range(4)]
                                + [(a, b, c_) for a in ("qJn", "qJr") for b in (0, 1) for c_ in (0, 1)])
                    e.memset(kA[96:128, :], 0.0); e.memset(kBt[96:128, :], 0.0)
                    ins = e.memset(qJ[96:128, :, :, :], 0.0); k.end(ins)
                    for kt, kn in ((kA, "kA"), (kBt, "kB")):
                        e = k.begin("dve", kr_all, [kn + "r"] + [(kn, t) for t in range(4)])
                        ins = e.tensor_copy(out=kt[64:96, :], in_=kr[64:96, :]); k.end(ins)
                    for tt in range(4):
                        for hd, kt, kn in ((0, kA, "kA"), (1, kBt, "kB")):
                            pb_ = gen()
                            mm_group(ps[pb_][0:64, :], ("ps", pb_),
                                     [(wukv[:, kc, hd, 0:64], ckvn[:, kc, tsl(tt)]) for kc in range(2)],
                                     ["wukv", ("ckvn", tt)])
                            e = k.begin("dve", [("ps", pb_)], [(kn, tt)])
                            ins = e.tensor_copy(out=kt[0:64, tsl(tt)], in_=ps[pb_][0:64, :]); k.end(ins)
                    for g in range(4):
                        pb_ = gen()
                        e = k.begin("pe", ["wukv", ("ckvn", g)], [("ps", pb_)])
                        for tb in range(4):
                            t0 = (g * 4 + tb) * 128
                            for kc in range(2):
                                ins = k.mm(e, ps[pb_][:, tb * 128:(tb + 1) * 128].rearrange("p (a b) -> p a b", b=64),
                                               ckvn[:, kc, t0:t0 + 128],
                                               wukv[:, kc, :, 64:128], start=(kc == 0), stop=(kc == 1))
                        k.end(ins)
                        v_evac(pb_, g, True)
                    if c == 0: chk(4)

                    def mla_q(qt, jb):
                        for hd in (0, 1):
                            pa = gen(); pb_ = gen()
                            mm_group(ps[pa][0:96, :], ("ps", pa),
                                     [(wuq[:, kc, hd * 96:hd * 96 + 96], cqn[:, kc, tsl(qt)]) for kc in range(3)],
                                     ["wuq", ("cqn", qt)])
                            mm_group(ps[pb_][0:96, :], ("ps", pb_),
                                     [(wuq[:, kc, 192 + hd * 96:192 + hd * 96 + 96], cqn[:, kc, tsl(qt)]) for kc in range(3)],
                                     ["wuq", ("cqn", qt)])
                            e = k.begin("dve", [("ps", pa)], [("qJn", jb, hd)])
                            ins = e.tensor_copy(out=qJ[0:64, jb, hd, :], in_=ps[pa][0:64, :]); k.end(ins)
                            rope_rows(pa, pb_, qJ[64:96, jb, hd, :], ("qJr", jb, hd), qt)

                    k.mark('mla_att%d' % c)
                    mla_q(0, 0)
                    if c == 0: chk(5)
                    for qt in range(4):
                        jb = qt % 2
                        if qt < 3: mla_q(qt + 1, (qt + 1) % 2)
                        sg_r = gate_for(r_ga, qt)
                        XA = 4 + 2 * (qt % 2); XB = XA + 1
                        streams = []
                        for hd, kt, kn, xb_, c0 in ((0, kA, "kA", XA, 0), (1, kBt, "kB", XB, 64)):
                            streams.append({
                                "k": (lambda kb, kt=kt: kt[:, kb * 128:(kb + 1) * 128]),
                                "q": qJ[:, jb, hd, :],
                                "kreads": (lambda kb, kn=kn: [(kn, kb // 4), kn + "r"]),
                                "qreads": [("qJn", jb, hd), ("qJr", jb, hd)],
                                "vreads": (lambda kb: [("V", kb // 4)]),
                                "acc": [(xb_, (lambda kb, c0=c0: Vt[:, kb, c0:c0 + 128]))],
                            })
                        attention(streams, list(range(16)), qt, 96.0 ** -0.5, lambda kb, st: ("N",))
                        finish_pair(XA, XB, sg_r, qt, "set")
                        if c == 0 and qt == 0: chk(6)
                    if c == 0: chk(7)

                    def kv_proj(r_k, r_v, padded):
                        for tt in range(4):
                            pb_ = gen()
                            slab_mm(pb_, r_k, tt)
                            e = k.begin("dve", [("ps", pb_)], [("kA", tt)])
                            ins = e.tensor_copy(out=kA[:, tsl(tt)], in_=ps[pb_][:]); k.end(ins)
                        for g in range(4):
                            pb_ = gen()
                            e = k.begin("pe", [("wsl", r_v), ("uT", g)], [("ps", pb_)])
                            for tb in range(4):
                                t0 = (g * 4 + tb) * 128
                                for kc in range(8):
                                    ins = k.mm(e, ps[pb_][:, tb * 128:(tb + 1) * 128], uT[:, kc, t0:t0 + 128],
                                                   wsl[:, r_v, kc * 128:(kc + 1) * 128], start=(kc == 0), stop=(kc == 7))
                            k.end(ins)
                            v_evac(pb_, g, padded)

                    def q_proj(r_q, qt, jb):
                        pb_ = gen()
                        slab_mm(pb_, r_q, qt)
                        e = k.begin("dve", [("ps", pb_)], [("qJn", jb, 0), ("qJr", jb, 0), ("qJn", jb, 1), ("qJr", jb, 1)])
                        e.tensor_copy(out=qJ[0:64, jb, 0, :], in_=ps[pb_][0:64, :])
                        ins = e.tensor_copy(out=qJ[64:128, jb, 1, :], in_=ps[pb_][64:128, :]); k.end(ins)

                    def half_stream(lo, jb, acc_):
                        slot = 0 if lo == 0 else 1
                        return {
                            "k": (lambda kb: kA[:, kb * 128:(kb + 1) * 128]),
                            "q": qJ[:, jb, slot, :],
                            "kreads": (lambda kb: [("kA", kb // 4)]),
                            "qreads": [("qJn", jb, slot), ("qJr", jb, slot)],
                            "vreads": (lambda kb: [("V", kb // 4)]),
                            "acc": acc_, "lo": lo,
                        }

                    def q_zero():
                        e = k.begin("dve", [], [(a, b, c_) for a in ("qJn", "qJr") for b in (0, 1) for c_ in (0, 1)])
                        e.memset(qJ[64:128, :, 0, :], 0.0)
                        ins = e.memset(qJ[0:64, :, 1, :], 0.0); k.end(ins)

                    k.mark('swa%d' % c)
                    r_q = load_slab(base + 0); r_k = load_slab(base + 1); r_v = load_slab(base + 2)
                    k.dma("sp", ("G", 0), [], [("G", 0)], [(G[:, 0, :], gm_d[2 * c])])
                    k.dma("sp", ("G", 1), [], [("G", 1)], [(G[:, 1, :], gm_d[2 * c + 1])])
                    kv_proj(r_k, r_v, True)
                    if c == 0: chk(8)
                    q_zero()
                    r_gb = load_slab(base + 7)
                    q_proj(r_q, 0, 0)
                    for qt in range(4):
                        jb = qt % 2
                        if qt < 3: q_proj(r_q, qt + 1, (qt + 1) % 2)
                        sg_r = gate_for(r_gb, qt)
                        XA = 4 + 2 * (qt % 2); XB = XA + 1
                        streams = [half_stream(0, jb, [(XA, lambda kb: Vt[:, kb, 0:128])]),
                                   half_stream(64, jb, [(XB, lambda kb: Vt[:, kb, 64:192])])]
                        kbs = [kb for kb in range(4 * qt - 1, 4 * qt + 5) if 0 <= kb < 16]
                        attention(streams, kbs, qt, 0.125,
                                  lambda kb, st, qt=qt: ("G", 0 if st["lo"] == 0 else 1, 512 - 128 * (kb - 4 * qt)))
                        finish_pair(XA, XB, sg_r, qt, "add", sinkcol=24 + c)
                    if c == 0: chk(9)

                    k.mark('diff%d' % c)
                    r_q = load_slab(base + 3); r_k = load_slab(base + 4); r_v = load_slab(base + 5)
                    k.dma("sp", ("G", 0), [], [("G", 0)], [(G[:, 0, :], gm_d[16 + c])])
                    kv_proj(r_k, r_v, False)
                    if c == 0: chk(10)
                    r_gc = load_slab(base + 8)
                    q_proj(r_q, 0, 0)

                    def dbias(kb, st, qt):
                        d_ = kb - 4 * qt
                        if -1 <= d_ <= 4:
                            return ("G", 0, 512 - 128 * d_)
                        return ("C", cst[:, C_CL + c:C_CL + c + 1] if d_ < 0 else cst[:, C_CR + c:C_CR + c + 1])
                    def diff_order(qt):
                        near = [kb for kb in range(16) if -1 <= kb - 4 * qt <= 4]
                        far = [kb for kb in range(16) if kb not in near]
                        out = []
                        while near or far:
                            if far: out.append(far.pop(0))
                            if near: out.append(near.pop(0))
                        return out
                    for qt in range(4):
                        jb = qt % 2
                        if qt < 3: q_proj(r_q, qt + 1, (qt + 1) % 2)
                        sg_r = gate_for(r_gc, qt)
                        X = 4; Y = 5
                        attention([half_stream(0, jb, [(X, lambda kb: Vt[:, kb, 0:128]), (Y, lambda kb: ones[:])])], diff_order(qt), qt, 0.125,
                                  lambda kb, st, qt=qt: dbias(kb, st, qt))
                        r1 = finish(X, Y, None, qt, "raw")
                        e = k.begin("dve", [("tmp", r1)], ["o1"])
                        ins = e.tensor_copy(out=o1[:], in_=tmp[:, r1, :]); k.end(ins)
                        X = 6; Y = 7
                        attention([half_stream(64, jb, [(X, lambda kb: Vt[:, kb, 0:128]), (Y, lambda kb: ones[:])])], diff_order(qt), qt, 0.125,
                                  lambda kb, st, qt=qt: dbias(kb, st, qt))
                        r2 = finish(X, Y, None, qt, "raw")
                        e = k.begin("dve", [("tmp", r2), "o1", "neglam"], ["o1"])
                        ins = e.scalar_tensor_tensor(out=o1[:], in0=tmp[:, r2, :], scalar=misc[:, 4:5], in1=o1[:],
                                                     op0=ALU.mult, op1=ALU.add); k.end(ins)
                        r = nxt("sq", 2)
                        e = k.begin("act", ["o1"], [("sq", r)])
                        ins = e.activation(out=sq[:, r, :], in_=o1[:], func=AF.Square, scale=1.0 / math.sqrt(128.0)); k.end(ins)
                        mb = gen()
                        mm_group(ps[mb][:], ("ps", mb), [(ones[:], sq[:, r, :])], [("sq", r), "ones"])
                        rstd_from(mb, rstd[:], "rstd")
                        e = k.begin("dve", ["o1", "rstd", "cst"], ["o1"])
                        ins = e.scalar_tensor_tensor(out=o1[:], in0=o1[:], scalar=cst[:, C_SUBLN:C_SUBLN + 1], in1=rstd[:],
                                                     op0=ALU.mult, op1=ALU.mult); k.end(ins)
                        e = k.begin("dve", ["o1", ("sgt", sg_r)], ["o1"])
                        ins = e.tensor_tensor(out=o1[:], in0=o1[:], in1=sgt[:, sg_r, :], op=ALU.mult); k.end(ins)
                        e = k.begin("dve", ["o1", ("macc", qt)], [("macc", qt)])
                        ins = e.scalar_tensor_tensor(out=macc[:, tsl(qt)], in0=o1[:], scalar=1.0 - lam_init, in1=macc[:, tsl(qt)],
                                                     op0=ALU.mult, op1=ALU.add); k.end(ins)

                    if c == 0: chk(11)
                    k.mark('wout%d' % c)
                    wb = nxt("wout", 2)
                    k.dma("pool", ("wout", wb), [], [("wout", wb)], [(wout[:, wb, :], wout_d[l, c * 128:(c + 1) * 128, :])])
                    for tt in range(4):
                        r = 0
                        e = k.begin("act", [("macc", tt)], [("mt", r)])
                        ins = e.activation(out=mt[:, r, :], in_=macc[:, tsl(tt)], func=AF.Copy); k.end(ins)
                        for dc in range(8):
                            ob = gen()
                            mm_group(ps[ob][:], ("ps", ob), [(wout[:, wb, dc * 128:(dc + 1) * 128], mt[:, r, :])],
                                     [("wout", wb), ("mt", r)])
                            e = k.begin("dve", [("ps", ob), ("hT", dc, tt)], [("hT", dc, tt)])
                            ins = e.tensor_tensor(out=hT[:, dc, tsl(tt)], in0=ps[ob][:], in1=hT[:, dc, tsl(tt)], op=ALU.add); k.end(ins)
                k.barrier()

        out_keys = []
        for s in range(nseq):
            k.dma("sp", "xin", [], [("hT", kc, tt) for kc in range(8) for tt in range(4)],
                  [(hT[:, kc, :], xT[s, kc * 128:(kc + 1) * 128, :]) for kc in range(8)])
            for l in range(depth):
                k.dma("sp", "cst", [], ["cst"], [(cst[:], cst_d[l])])
                if "ffn1" in phases: ffn(l, 0, C_FFN1)
                if "mix" in phases: mixer(l)
                if "ffn2" in phases: ffn(l, 1, C_FFN2)
                if "ple" in phases: ple(l, s)
            with ExitStack() as ph:
                ot = sb(ph, "ot", [128, 2, 512], F32)
                rot["ot"] = 0

                def fo(kc, tt):
                    r = nxt("ot", 2)
                    e = k.begin("dve", [("hT", kc, tt), "rstd", "cst"], [("ot", r)])
                    ins = e.scalar_tensor_tensor(out=ot[:, r, :], in0=hT[:, kc, tsl(tt)],
                                                 scalar=cst[:, C_FIN + kc:C_FIN + kc + 1], in1=rstd[:],
                                                 op0=ALU.mult, op1=ALU.mult); k.end(ins)
                    k.dma("sp", ("ot", r), [("ot", r)], [], [(yT[s, kc * 128:(kc + 1) * 128, tsl(tt)], ot[:, r, :])])
                    if ("ot", r) not in out_keys: out_keys.append(("ot", r))
                norm(C_FIN, fo)
                k.barrier()
        if "dbg" in k.dsem: out_keys.append("dbg")
        for key in out_keys:
            nc.sync.wait_ge(k.dsem[key], k.dcnt[key])
    LAST_MARKS[:] = k.marks + [('end', k.nmm)]
    return nc


_NC_CACHE = {}


def kernel(**inputs):
    inp = {kk: np.asarray(v, dtype=np.float32) for kk, v in inputs.items()}
    shared = _prep_shared(inp)
    xT = np.ascontiguousarray(inp["x"].transpose(0, 2, 1))
    pT = np.ascontiguousarray(inp["p"].transpose(0, 1, 3, 2))
    if "nc" not in _NC_CACHE:
        _NC_CACHE["nc"] = build()
    nc = _NC_CACHE["nc"]
    in_maps = []
    for c in range(NCORE):
        m = dict(shared)
        m["xT"] = xT[c * NSEQ:(c + 1) * NSEQ]
        m["pT"] = np.ascontiguousarray(pT[:, c * NSEQ:(c + 1) * NSEQ])
        in_maps.append(m)
    res = run_bass_kernel_spmd(nc, in_maps, core_ids=list(range(NCORE)))
    yT = np.concatenate([r["yT"] for r in res.results], axis=0)
    return np.ascontiguousarray(yT.transpose(0, 2, 1)).astype(np.float32)
```
